# Optimizing a Trainium2 kernel written in Bass

```python
import jax, jax.numpy as jnp
from jax import lax
import numpy as np

D_MODEL = 1024
BATCH = 32
SEQ = 256
DEPTH = 2
DEC_BATCH = 8
DEC_SEQ = 1024
PAST_LEN = 512

GRID_W = 64
HEAD_DIM = 64
BRANCH_W = 512
A_HEADS = BRANCH_W // HEAD_DIM
A_KV_HEADS = 2
A_KV_W = A_KV_HEADS * HEAD_DIM
A_WINDOW = 128
A_BLOCK = 128
B_HEADS = BRANCH_W // HEAD_DIM
NB_ROWS = 8
NB_COLS = 16
NB_QCOLS = 16
NB_KCOLS = 32
LRU_WIDTH = BRANCH_W
LRU_BLOCKS = 8
LRU_BW = LRU_WIDTH // LRU_BLOCKS
LRU_C = 8.0
CONV_W = 4
N_BRANCH = 3
Q_BLOCK = 128
ROPE_BASE = 10000.0
EPS = 1e-6
NEG_INF = -1e30
IN_SPLITS = (BRANCH_W, A_KV_W, A_KV_W, BRANCH_W,
             BRANCH_W, BRANCH_W, BRANCH_W, BRANCH_W,
             LRU_WIDTH, LRU_WIDTH,
             D_MODEL, D_MODEL, D_MODEL)
IN_COLS = sum(IN_SPLITS)
IN_OFFSETS = tuple(int(v) for v in np.cumsum(IN_SPLITS)[:-1])

kernel_name = "hybrid_diffusion_prefix_trunk_step"


def rms_norm(x, g):
    x32 = x.astype(jnp.float32)
    y = x32 * lax.rsqrt(jnp.mean(x32 * x32, axis=-1, keepdims=True) + EPS)
    return (y * g.astype(jnp.float32)).astype(x.dtype)


def mixer_inputs(x, cond, norm_g, w_ada, b_ada, w_in):
    mod = jax.nn.silu(cond) @ w_ada + b_ada
    if mod.ndim == 2:
        mod = mod[:, None, :]
    shift, scale, gate = jnp.split(mod, 3, axis=-1)
    h = rms_norm(x, norm_g) * (1 + scale) + shift
    return jnp.split(h @ w_in, IN_OFFSETS, axis=-1), gate


def rope_axis(x, pos):
    m = x.shape[-1] // 2
    freqs = ROPE_BASE ** (-jnp.arange(m, dtype=jnp.float32) / m)
    ang = pos.astype(jnp.float32)[:, None] * freqs[None, :]
    cos = jnp.cos(ang)[None, :, None, :].astype(x.dtype)
    sin = jnp.sin(ang)[None, :, None, :].astype(x.dtype)
    x1, x2 = x[..., :m], x[..., m:]
    return jnp.concatenate([x1 * cos - x2 * sin, x2 * cos + x1 * sin], axis=-1)


def rope_2d(x):
    t = jnp.arange(x.shape[1])
    half = x.shape[-1] // 2
    return jnp.concatenate([rope_axis(x[..., :half], t // GRID_W),
                            rope_axis(x[..., half:], t % GRID_W)], axis=-1)


def context_attention(q, k, v, sink):
    bsz, s_len, hq, dh = q.shape
    hkv = k.shape[2]
    g = hq // hkv
    nb = s_len // Q_BLOCK
    scale = dh ** -0.5
    qb = jnp.moveaxis(q.reshape(bsz, nb, Q_BLOCK, hkv, g, dh), 1, 0)

    def block(qi):
        s = jnp.einsum('bqhgd,bkhd->bhgqk', qi, k).astype(jnp.float32) * scale
        if sink is not None:
            sk = jnp.broadcast_to(sink.astype(jnp.float32).reshape(1, hkv, g, 1, 1), s.shape[:-1] + (1,))
            s = jnp.concatenate([s, sk], axis=-1)
        p = jax.nn.softmax(s, axis=-1)[..., :s_len].astype(v.dtype)
        return jnp.einsum('bhgqk,bkhd->bqhgd', p, v)

    o = lax.map(block, qb)
    return jnp.moveaxis(o, 0, 1).reshape(bsz, s_len, hq * dh)


def window_attention(q, k, v, kc, vc, sink):
    bsz, t_len, hq, dh = q.shape
    hkv = k.shape[2]
    g = hq // hkv
    nb = t_len // A_BLOCK
    nloc = 3 * A_BLOCK
    scale = dh ** -0.5
    qb = jnp.moveaxis(q.reshape(bsz, nb, A_BLOCK, hkv, g, dh), 1, 0)

    def band(x):
        xp = jnp.pad(x.reshape(bsz, nb, A_BLOCK, hkv, dh), ((0, 0), (1, 1), (0, 0), (0, 0), (0, 0)))
        xb = jnp.concatenate([xp[:, :-2], xp[:, 1:-1], xp[:, 2:]], axis=2)
        return jnp.moveaxis(xb, 1, 0)

    kb, vb = band(k), band(v)
    blk = jnp.arange(nb)[:, None, None]
    qpos = blk * A_BLOCK + jnp.arange(A_BLOCK)[None, :, None]
    kpos = (blk - 1) * A_BLOCK + jnp.arange(nloc)[None, None, :]
    valid = (jnp.abs(qpos - kpos) <= A_WINDOW) & (kpos >= 0) & (kpos < t_len)
    sk = sink.astype(jnp.float32).reshape(1, hkv, g, 1, 1)

    def block(args):
        qi, ki, vi, mi = args
        s_loc = jnp.einsum('bqhgd,bkhd->bhgqk', qi, ki).astype(jnp.float32) * scale
        s_loc = jnp.where(mi, s_loc, NEG_INF)
        s_ctx = jnp.einsum('bqhgd,bphd->bhgqp', qi, kc).astype(jnp.float32) * scale
        s_snk = jnp.broadcast_to(sk, s_loc.shape[:-1] + (1,))
        p = jax.nn.softmax(jnp.concatenate([s_loc, s_ctx, s_snk], axis=-1), axis=-1).astype(v.dtype)
        return (jnp.einsum('bhgqk,bkhd->bqhgd', p[..., :nloc], vi)
                + jnp.einsum('bhgqp,bphd->bqhgd', p[..., nloc:-1], vc))

    o = lax.map(block, (qb, kb, vb, valid))
    return jnp.moveaxis(o, 0, 1).reshape(bsz, t_len, hq * dh)


def neighbourhood_attention(q, k, v, kc, vc, rpb):
    bsz, t_len, h, dh = q.shape
    rows = t_len // GRID_W
    kh = min(NB_ROWS, rows)
    ncb = GRID_W // NB_QCOLS
    nkeys = kh * NB_KCOLS
    scale = dh ** -0.5
    row_start = jnp.clip(jnp.arange(rows) - kh // 2, 0, rows - kh)
    kcol0 = jnp.clip(jnp.arange(ncb) * NB_QCOLS - (NB_KCOLS - NB_QCOLS) // 2, 0, GRID_W - NB_KCOLS)
    kcol = kcol0[:, None] + jnp.arange(NB_KCOLS)[None, :]
    qcol = jnp.arange(ncb)[:, None] * NB_QCOLS + jnp.arange(NB_QCOLS)[None, :]
    cs = jnp.clip(qcol - NB_COLS // 2, 0, GRID_W - NB_COLS)
    col_ok = (kcol[:, None, :] >= cs[:, :, None]) & (kcol[:, None, :] < cs[:, :, None] + NB_COLS)
    mask = jnp.broadcast_to(col_ok[:, :, None, :], (ncb, NB_QCOLS, kh, NB_KCOLS)).reshape(ncb, NB_QCOLS, nkeys)
    dc_idx = jnp.clip(kcol[:, None, :] - qcol[:, :, None] + NB_COLS - 1, 0, 2 * NB_COLS - 2)
    qr = jnp.moveaxis(q.reshape(bsz, rows, ncb, NB_QCOLS, h, dh), 1, 0)

    def block(args):
        qi, r = args
        krow = row_start[r] + jnp.arange(kh)
        idx = (krow[None, :, None] * GRID_W + kcol[:, None, :]).reshape(-1)
        ki = jnp.take(k, idx, axis=1).reshape(bsz, ncb, nkeys, h, dh)
        vi = jnp.take(v, idx, axis=1).reshape(bsz, ncb, nkeys, h, dh)
        dr_idx = krow - r + NB_ROWS - 1
        bias = rpb[:, dr_idx[None, None, :, None], dc_idx[:, :, None, :]]
        bias = jnp.moveaxis(bias.reshape(h, ncb, NB_QCOLS, nkeys), 0, 1)
        s_loc = jnp.einsum('bjqhd,bjkhd->bjhqk', qi, ki).astype(jnp.float32) * scale + bias.astype(jnp.float32)
        s_loc = jnp.where(mask[:, None], s_loc, NEG_INF)
        s_ctx = jnp.einsum('bjqhd,bphd->bjhqp', qi, kc).astype(jnp.float32) * scale
        p = jax.nn.softmax(jnp.concatenate([s_loc, s_ctx], axis=-1), axis=-1).astype(v.dtype)
        o = (jnp.einsum('bjhqk,bjkhd->bjqhd', p[..., :nkeys], vi)
             + jnp.einsum('bjhqp,bphd->bjqhd', p[..., nkeys:], vc))
        return o.reshape(bsz, GRID_W, h * dh)

    o = lax.map(block, (qr, jnp.arange(rows)))
    return jnp.moveaxis(o, 0, 1).reshape(bsz, t_len, h * dh)


def depthwise_conv(x, w, b):
    t_len = x.shape[1]
    lo = CONV_W // 2
    xp = jnp.pad(x, ((0, 0), (lo, CONV_W - 1 - lo), (0, 0)))
    y = b
    for j in range(CONV_W):
        y = y + xp[:, j:j + t_len] * w[j]
    return y


def rglru_coeffs(xc, wa, ba, wx, bx, lam):
    bsz, t_len, _ = xc.shape
    xb = xc.reshape(bsz, t_len, LRU_BLOCKS, LRU_BW)
    r = jax.nn.sigmoid((jnp.einsum('btnk,dnkj->dbtnj', xb, wa).reshape(2, bsz, t_len, LRU_WIDTH)
                        + ba[:, None, None, :]).astype(jnp.float32))
    i = jax.nn.sigmoid((jnp.einsum('btnk,dnkj->dbtnj', xb, wx).reshape(2, bsz, t_len, LRU_WIDTH)
                        + bx[:, None, None, :]).astype(jnp.float32))
    log_a = -LRU_C * jax.nn.softplus(-lam.astype(jnp.float32))[:, None, None, :] * r
    a = jnp.exp(log_a)
    u = jnp.sqrt(-jnp.expm1(2.0 * log_a)) * i * xc.astype(jnp.float32)[None]
    return a, u


def _combine(left, right):
    a1, b1 = left
    a2, b2 = right
    return a1 * a2, a2 * b1 + b2


def linear_scan(a, u, h0, reverse):
    acum, ucum = lax.associative_scan(_combine, (a, u), axis=1, reverse=reverse)
    return acum * h0[:, None, :] + ucum


def merge_branches(ya, yb, yc, ga, gb, gc, w_br, w_out):
    z = (jax.nn.sigmoid(ga) * (ya @ w_br[0]) + jax.nn.sigmoid(gb) * (yb @ w_br[1])
         + jax.nn.sigmoid(gc) * (yc @ w_br[2]))
    return z @ w_out


def context_layer(x, c_ctx, norm_g, w_ada, b_ada, w_in, a_qn, a_kn, a_sink, b_qn, b_kn,
                  conv_w, conv_b, wa, ba, wx, bx, lam, w_br, w_out):
    (aq, ak, av, ag, bq, bk, bv, bg, cx, cg, ga, gb, gc), gate = mixer_inputs(x, c_ctx, norm_g, w_ada, b_ada, w_in)
    bsz, s_len, _ = x.shape
    qa = rms_norm(aq.reshape(bsz, s_len, A_HEADS, HEAD_DIM), a_qn)
    ka = rms_norm(ak.reshape(bsz, s_len, A_KV_HEADS, HEAD_DIM), a_kn)
    va = av.reshape(bsz, s_len, A_KV_HEADS, HEAD_DIM)
    ya = context_attention(qa, ka, va, a_sink) * jax.nn.silu(ag)
    qb = rms_norm(bq.reshape(bsz, s_len, B_HEADS, HEAD_DIM), b_qn)
    kb = rms_norm(bk.reshape(bsz, s_len, B_HEADS, HEAD_DIM), b_kn)
    vb = bv.reshape(bsz, s_len, B_HEADS, HEAD_DIM)
    yb = context_attention(qb, kb, vb, None) * jax.nn.silu(bg)
    xc = depthwise_conv(cx, conv_w, conv_b)
    a, u = rglru_coeffs(xc, wa, ba, wx, bx, lam)
    h0 = jnp.zeros((bsz, LRU_WIDTH), jnp.float32)
    hf = linear_scan(a[0], u[0], h0, False)
    hb = linear_scan(a[1], u[1], h0, True)
    yc = (hf + hb).astype(x.dtype) * jax.nn.silu(cg)
    lru_state = jnp.stack([hf[:, -1], hb[:, 0]], axis=1)
    out = merge_branches(ya, yb, yc, ga, gb, gc, w_br, w_out)
    return x + gate * out, ka, va, kb, vb, lru_state


def latent_layer(x, c, ka_c, va_c, kb_c, vb_c, st, norm_g, w_ada, b_ada, w_in, a_qn, a_kn, a_sink,
                 b_qn, b_kn, b_rpb, conv_w, conv_b, wa, ba, wx, bx, lam, w_br, w_out):
    (aq, ak, av, ag, bq, bk, bv, bg, cx, cg, ga, gb, gc), gate = mixer_inputs(x, c, norm_g, w_ada, b_ada, w_in)
    bsz, t_len, _ = x.shape
    qa = rope_2d(rms_norm(aq.reshape(bsz, t_len, A_HEADS, HEAD_DIM), a_qn))
    ka = rope_2d(rms_norm(ak.reshape(bsz, t_len, A_KV_HEADS, HEAD_DIM), a_kn))
    va = av.reshape(bsz, t_len, A_KV_HEADS, HEAD_DIM)
    ya = window_attention(qa, ka, va, ka_c, va_c, a_sink) * jax.nn.silu(ag)
    qb = rms_norm(bq.reshape(bsz, t_len, B_HEADS, HEAD_DIM), b_qn)
    kb = rms_norm(bk.reshape(bsz, t_len, B_HEADS, HEAD_DIM), b_kn)
    vb = bv.reshape(bsz, t_len, B_HEADS, HEAD_DIM)
    yb = neighbourhood_attention(qb, kb, vb, kb_c, vb_c, b_rpb) * jax.nn.silu(bg)
    xc = depthwise_conv(cx, conv_w, conv_b)
    a, u = rglru_coeffs(xc, wa, ba, wx, bx, lam)
    st32 = st.astype(jnp.float32)
    hf = linear_scan(a[0], u[0], st32[:, 0], False)
    hb = linear_scan(a[1], u[1], st32[:, 1], True)
    yc = (hf + hb).astype(x.dtype) * jax.nn.silu(cg)
    out = merge_branches(ya, yb, yc, ga, gb, gc, w_br, w_out)
    return x + gate * out


def setup_inputs(seed: int = 0) -> dict:
    key = jax.random.key(seed)
    ks = jax.random.split(key, 32)
    f32 = jnp.float32
    nrm = lambda k, shape, s: jax.random.normal(k, shape, f32) * s
    lam_u = jax.random.uniform(ks[24], (DEPTH, 2, LRU_WIDTH), f32, minval=0.9, maxval=0.999)
    return {
        "x_prompt": nrm(ks[0], (BATCH, SEQ, D_MODEL), 1.0),
        "x_sample": nrm(ks[1], (DEC_BATCH, DEC_SEQ, D_MODEL), 1.0),
        "cache_ka": nrm(ks[2], (DEC_BATCH, DEPTH, PAST_LEN, A_KV_HEADS, HEAD_DIM), 1.0),
        "cache_va": nrm(ks[3], (DEC_BATCH, DEPTH, PAST_LEN, A_KV_HEADS, HEAD_DIM), 1.0),
        "cache_kb": nrm(ks[4], (DEC_BATCH, DEPTH, PAST_LEN, B_HEADS, HEAD_DIM), 1.0),
        "cache_vb": nrm(ks[5], (DEC_BATCH, DEPTH, PAST_LEN, B_HEADS, HEAD_DIM), 1.0),
        "state_lru": nrm(ks[6], (DEC_BATCH, DEPTH, 2, LRU_WIDTH), 0.5),
        "c": nrm(ks[7], (DEC_BATCH, D_MODEL), 1.0),
        "c_ctx": nrm(ks[8], (D_MODEL,), 1.0),
        "norm_g": 1.0 + nrm(ks[9], (DEPTH, D_MODEL), 0.02),
        "w_ada": nrm(ks[10], (DEPTH, D_MODEL, 3 * D_MODEL), D_MODEL ** -0.5),
        "b_ada": nrm(ks[11], (DEPTH, 3 * D_MODEL), 0.02),
        "w_in": nrm(ks[12], (DEPTH, D_MODEL, IN_COLS), D_MODEL ** -0.5),
        "a_q_norm": 1.0 + nrm(ks[13], (DEPTH, HEAD_DIM), 0.02),
        "a_k_norm": 1.0 + nrm(ks[14], (DEPTH, HEAD_DIM), 0.02),
        "a_sink": nrm(ks[15], (DEPTH, A_HEADS), 0.5),
        "b_q_norm": 1.0 + nrm(ks[16], (DEPTH, HEAD_DIM), 0.02),
        "b_k_norm": 1.0 + nrm(ks[17], (DEPTH, HEAD_DIM), 0.02),
        "b_rpb": nrm(ks[18], (DEPTH, B_HEADS, 2 * NB_ROWS - 1, 2 * NB_COLS - 1), 0.1),
        "lru_conv_w": nrm(ks[19], (DEPTH, CONV_W, LRU_WIDTH), CONV_W ** -0.5),
        "lru_conv_b": nrm(ks[20], (DEPTH, LRU_WIDTH), 0.02),
        "lru_wa": nrm(ks[21], (DEPTH, 2, LRU_BLOCKS, LRU_BW, LRU_BW), LRU_BW ** -0.5),
        "lru_ba": nrm(ks[22], (DEPTH, 2, LRU_WIDTH), 0.02),
        "lru_wx": nrm(ks[23], (DEPTH, 2, LRU_BLOCKS, LRU_BW, LRU_BW), LRU_BW ** -0.5),
        "lru_bx": nrm(ks[25], (DEPTH, 2, LRU_WIDTH), 0.02),
        "lru_lambda": jnp.log(lam_u) - jnp.log1p(-lam_u),
        "w_branch": nrm(ks[26], (DEPTH, N_BRANCH, BRANCH_W, D_MODEL), BRANCH_W ** -0.5),
        "w_out": nrm(ks[27], (DEPTH, D_MODEL, D_MODEL), D_MODEL ** -0.5),
    }


def reference(x_prompt, x_sample, cache_ka, cache_va, cache_kb, cache_vb, state_lru, c, c_ctx,
              norm_g, w_ada, b_ada, w_in, a_q_norm, a_k_norm, a_sink, b_q_norm, b_k_norm, b_rpb,
              lru_conv_w, lru_conv_b, lru_wa, lru_ba, lru_wx, lru_bx, lru_lambda, w_branch, w_out):
    y_prompt = x_prompt
    y_sample = x_sample
    new_ka, new_va, new_kb, new_vb, new_lru = [], [], [], [], []
    for l in range(DEPTH):
        y_prompt, ka, va, kb, vb, st = context_layer(
            y_prompt, c_ctx, norm_g[l], w_ada[l], b_ada[l], w_in[l], a_q_norm[l], a_k_norm[l], a_sink[l],
            b_q_norm[l], b_k_norm[l], lru_conv_w[l], lru_conv_b[l], lru_wa[l], lru_ba[l], lru_wx[l],
            lru_bx[l], lru_lambda[l], w_branch[l], w_out[l])
        new_ka.append(ka)
        new_va.append(va)
        new_kb.append(kb)
        new_vb.append(vb)
        new_lru.append(st)
        y_sample = latent_layer(
            y_sample, c, cache_ka[:, l], cache_va[:, l], cache_kb[:, l], cache_vb[:, l], state_lru[:, l],
            norm_g[l], w_ada[l], b_ada[l], w_in[l], a_q_norm[l], a_k_norm[l], a_sink[l],
            b_q_norm[l], b_k_norm[l], b_rpb[l], lru_conv_w[l], lru_conv_b[l], lru_wa[l], lru_ba[l],
            lru_wx[l], lru_bx[l], lru_lambda[l], w_branch[l], w_out[l])
    new_cache_ka = jnp.stack(new_ka, axis=1)
    new_cache_va = jnp.stack(new_va, axis=1)
    new_cache_kb = jnp.stack(new_kb, axis=1)
    new_cache_vb = jnp.stack(new_vb, axis=1)
    new_state_lru = jnp.stack(new_lru, axis=1)
    return (y_prompt, y_sample, new_cache_ka, new_cache_va, new_cache_kb, new_cache_vb, new_state_lru)
```

```python
import numpy as np
import ml_dtypes
import concourse.bass as bass
import concourse.mybir as mybir
from concourse.bass_utils import run_bass_kernel_spmd

F32 = mybir.dt.float32
BF16 = mybir.dt.bfloat16
ALU = mybir.AluOpType
AF = mybir.ActivationFunctionType
AX = mybir.AxisListType

NCORES = 8
D = 1024
T = 1024
NT = 8
EPS = 1e-6
NEG = -30000.0
IN_COLS = 7424
HALVES = ("P", "S")


class KB:
    def __init__(self, nc, needed=None):
        self.nc = nc
        self.needed = needed
        self.rank = None if needed is None else {e: {idx: i + 1 for i, idx in enumerate(v)} for e, v in needed.items()}
        self.rec = {e: set() for e in ("pe", "act", "dve", "pool")}
        self.eng = {"pe": nc.tensor, "act": nc.scalar, "dve": nc.vector, "pool": nc.gpsimd, "sp": nc.sync}
        self.sem = {e: nc.alloc_semaphore("s_" + e) for e in ("pe", "act", "dve", "pool")}
        self.cnt = {e: 0 for e in self.sem}
        self.waited = {}
        self.lastw = {}
        self.readers = {}
        self.dq = {}
        self.NQ = 8
        for q in ("sp", "pool"):
            self.dq[q] = {"sems": [nc.alloc_semaphore(f"d_{q}{i}") for i in range(self.NQ)], "n": 0,
                          "val": [0] * self.NQ}
        self.nwaits = 0
        self.nops = 0
        self.alias = {}

    def ex(self, keys):
        out = []
        for k in keys:
            for a in self.alias.get(k, (k,)):
                if a not in out:
                    out.append(a)
        return out

    def _deps(self, reads, writes):
        toks = []
        for r in reads:
            t = self.lastw.get(r)
            if t is not None:
                toks.append(t)
        for w in writes:
            t = self.lastw.get(w)
            if t is not None:
                toks.append(t)
            toks.extend(self.readers.get(w, []))
        return toks

    def _wait(self, e, toks):
        best = {}
        for (sem, val, src) in toks:
            if e == "pe" and src == "pe":
                continue
            k = id(sem)
            if k not in best or best[k][1] < val:
                best[k] = (sem, val, src)
        for k, (sem, val, src) in best.items():
            if self.waited.get((e, k), 0) < val:
                wv = val
                if src != "dma":
                    self.rec[src].add(val)
                    if self.rank is not None:
                        wv = self.rank[src][val]
                self.eng[e].wait_ge(sem, wv)
                self.waited[(e, k)] = val
                self.nwaits += 1

    def _commit(self, tok, reads, writes):
        for w in writes:
            self.lastw[w] = tok
            self.readers[w] = []
        for r in reads:
            if r not in writes:
                self.readers.setdefault(r, []).append(tok)

    def op(self, e, fn, reads=(), writes=()):
        reads, writes = self.ex(reads), self.ex(writes)
        self._wait(e, self._deps(reads, writes))
        ins = fn(self.eng[e])
        self.cnt[e] += 1
        self.nops += 1
        if self.rank is None or self.cnt[e] in self.rank[e]:
            ins.then_inc(self.sem[e], 1)
        tok = (self.sem[e], self.cnt[e], e)
        self._commit(tok, reads, writes)
        return tok

    def dma(self, q, out, in_, reads=(), writes=(), **kw):
        reads, writes = self.ex(reads), self.ex(writes)
        d = self.dq[q]
        j = d["n"] % self.NQ
        d["n"] += 1
        sem = d["sems"][j]
        toks = self._deps(reads, writes)
        if d["val"][j] > 0:
            toks.append((sem, d["val"][j], "dma"))
        self._wait(q, toks)
        ins = self.eng[q].dma_start(out=out, in_=in_, **kw)
        d["val"][j] += 16
        ins.then_inc(sem, 16)
        tok = (sem, d["val"][j], "dma")
        self._commit(tok, reads, writes)
        self.nops += 1
        return tok

    def finish(self, e="sp"):
        toks = list(self.lastw.values())
        for l in self.readers.values():
            toks.extend(l)
        self._wait(e, toks)


def bc(ap, pos, n):
    l = [list(x) for x in ap.ap]
    return bass.AP(ap.tensor, ap.offset, l[:pos] + [[0, n]] + l[pos:])


def dram_bcast(ap, nparts=128):
    l = [list(x) for x in ap.ap]
    return bass.AP(ap.tensor, ap.offset, [[0, nparts]] + l)


def host_consts():
    c = {}
    c["ident"] = np.eye(128, dtype=np.float32).astype(ml_dtypes.bfloat16)
    pos = (np.arange(8)[None, :] * 128 + np.arange(128)[:, None]).astype(np.float64)
    freqs = 10000.0 ** (-np.arange(16, dtype=np.float64) / 16.0)
    row = np.floor(pos / 64.0)
    col = pos - row * 64.0
    ang = np.concatenate([row[:, :, None] * freqs[None, None, :], col[:, :, None] * freqs[None, None, :]], axis=2)
    ang32 = np.concatenate([(row.astype(np.float32)[:, :, None] * freqs.astype(np.float32)[None, None, :]),
                            (col.astype(np.float32)[:, :, None] * freqs.astype(np.float32)[None, None, :])], axis=2)
    c["ropec"] = np.cos(ang32.astype(np.float64)).astype(np.float32)
    c["ropes"] = np.sin(ang32.astype(np.float64)).astype(np.float32)
    k = np.arange(128)[:, None]
    q = np.arange(128)[None, :]
    mp = np.where(q <= k, 0.0, NEG).astype(np.float32)
    mn = np.where(k <= q, 0.0, NEG).astype(np.float32)
    c["wmask"] = np.stack([np.tile(mp, (1, 2)), np.tile(mn, (1, 2))], axis=1).astype(ml_dtypes.bfloat16)
    kc = np.arange(64)[:, None]
    qc = np.arange(64)[None, :]
    cs = np.clip(qc - 8, 0, 48)
    cm = np.where((kc >= cs) & (kc < cs + 16), 0.0, NEG).astype(np.float32)
    c["colm"] = np.concatenate([cm, cm], axis=0)
    rf = np.zeros((128, 128), np.float32)
    rf[0:64, 64:128] = NEG
    rl = np.full((128, 128), NEG, np.float32)
    rl[0:64, 64:128] = 0.0
    c["rmask"] = np.stack([rf, rl], axis=1).astype(ml_dtypes.bfloat16)
    return c


def nb_chunks(i):
    rows = [2 * i, 2 * i + 1]
    rs = [min(max(r - 4, 0), 8) for r in rows]
    out = []
    for c in range(8):
        kr = [2 * c, 2 * c + 1]
        valid = [[(rs[a] <= kr[b] < rs[a] + 8) for b in range(2)] for a in range(2)]
        flat = (valid[0][0], valid[0][1], valid[1][0], valid[1][1])
        if not any(flat):
            continue
        if all(flat):
            kind = None
        elif flat == (True, True, False, True):
            kind = 0
        elif flat == (False, False, True, False):
            kind = 1
        else:
            raise AssertionError(("unexpected row mask", i, c, flat))
        out.append((c, kind))
    return out


def build_program(halves=HALVES, debug=False):
    _, kb0 = _build(halves, debug, None)
    needed = {e: sorted(v) for e, v in kb0.rec.items()}
    nc, kb1 = _build(halves, debug, needed)
    print("signalling ops per engine:", {e: len(v) for e, v in needed.items()})
    return nc


def _build(halves, debug, needed):
    nc = bass.Bass("TRN2", target_bir_lowering=False)
    kb = KB(nc, needed)
    op, dma = kb.op, kb.dma
    dbg_done = {}

    def dbg(name, ap, keys):
        if not debug or name in dbg_done:
            return
        dbg_done[name] = True
        dt_ = nc.dram_tensor("dbg_" + name, list(ap.shape), ap.dtype, kind="ExternalOutput").ap()
        dma("sp", dt_, ap, reads=keys)

    def din(name, shape, dt=F32):
        return nc.dram_tensor(name, list(shape), dt, kind="ExternalInput").ap()

    def dout(name, shape):
        return nc.dram_tensor(name, list(shape), F32, kind="ExternalOutput").ap()

    d_xp = din("xp", [T, D])
    d_xs = din("xs", [T, D])
    d_cka = din("cka", [2, 512, 128])
    d_cva = din("cva", [2, 512, 128])
    d_ckb = din("ckb", [2, 512, 512])
    d_cvb = din("cvb", [2, 512, 512])
    d_st = din("st", [2, 2, 512])
    d_cond = din("cond", [2, D])
    d_normg = din("norm_g", [2, D])
    d_wada = din("w_ada", [2, D, 3 * D])
    d_bada = din("b_ada", [2, 3 * D])
    d_win = din("w_in", [2, D, IN_COLS])
    d_gqk = din("gqk", [4, 2, 64])
    d_sink = din("a_sink", [2, 8])
    d_rpb = din("rpbpad", [2 * 8 * 15 * 31 + 256])
    d_convw = din("conv_w", [2, 4, 512])
    d_convb = din("conv_b", [2, 512])
    d_wa = din("lru_wa", [2, 2, 8, 64, 64])
    d_ba = din("lru_ba", [2, 2, 512])
    d_wx = din("lru_wx", [2, 2, 8, 64, 64])
    d_bx = din("lru_bx", [2, 2, 512])
    d_lam = din("lru_lam", [2, 2, 512])
    d_wbr = din("w_br", [2, 3, 512, D])
    d_wout = din("w_out", [2, D, D])
    d_ident = din("ident", [128, 128], BF16)
    d_ropec = din("ropec", [128, 8, 32])
    d_ropes = din("ropes", [128, 8, 32])
    d_wmask = din("wmask", [128, 2, 256], BF16)
    d_colm = din("colm", [128, 64])
    d_rmask = din("rmask", [128, 2, 128], BF16)

    d_yp = dout("yp", [T, D])
    d_ys = dout("ys", [T, D])
    d_nka = dout("nka", [4, 2, 256, 128])
    d_nva = dout("nva", [4, 2, 256, 128])
    d_nkb = dout("nkb", [4, 2, 256, 512])
    d_nvb = dout("nvb", [4, 2, 256, 512])
    d_nst = dout("nst", [4, 2, 2, 512])
    d_modd = nc.dram_tensor("modd", [2, 2, 3 * D], F32, kind="Internal").ap()

    A = nc.alloc_sbuf_tensor
    X = A("X", [128, NT, D], F32)
    hT = A("hT", [128, 8, T], BF16)
    NR = 4
    RING = [A(f"ring{i}", [128, 4096], BF16) for i in range(NR)]
    yT01 = [A(f"yT{b}", [128, 4, T], BF16) for b in range(2)]
    RH = A("RH", [128, 5120], F32)
    yT = yT01 + [RH[:, 0:2048].bitcast(BF16).rearrange("p (k t) -> p k t", k=4)]
    WBR = RH[:, 2048:5120].bitcast(BF16).rearrange("p (b k c) -> p b k c", b=3, k=4)
    G = RH[:, 0:3840].bitcast(BF16).rearrange("p (h a q) -> p h a q", h=8, a=15)
    RA = A("RA", [128, 2048], F32)
    RB = A("RB", [128, 2048], F32)
    RC = A("RC", [128, 2080], F32)
    RD = A("RD", [128, 2048], F32)
    RE = A("RE", [128, 1792], F32)
    RF = A("RF", [128, 1040], F32)
    RG = A("RG", [128, 1040], F32)
    QT = RA[:, :].bitcast(BF16).rearrange("p (k t) -> p k t", k=4)
    KT = RB[:, :].bitcast(BF16).rearrange("p (k t) -> p k t", k=4)
    VA = RC[:, :].bitcast(BF16).rearrange("p (t h d) -> p t h d", t=NT, h=8)
    GT = RD[:, :].bitcast(BF16).rearrange("p (t c) -> p t c", t=NT)
    MODB = RD
    NPT = 1792
    PTB = [RE[:, 0:896].bitcast(BF16), RE[:, 896:1792].bitcast(BF16)]
    LR = [RA[:, 0:1024], RA[:, 1024:2048]]
    LI = [RB[:, 0:1024], RB[:, 1024:2048]]
    LH = [RC[:, 0:1024], RC[:, 1024:2048]]
    LA2 = [RD[:, 0:1024], RD[:, 1024:2048]]
    XC = RE[:, 0:1024]
    XCb = RE[:, 1024:1536].bitcast(BF16)
    KNO = [RF[:, 0:512], RF[:, 512:1024]]
    KCT = RF[:, 0:1024].bitcast(BF16).rearrange("p (k t) -> p k t", k=4)
    SG = RF[:, 0:1024]
    VO = [RG[:, 0:512], RG[:, 512:1024]]
    VC = RG[:, :].bitcast(BF16).rearrange("p (c h d) -> p c h d", c=4, h=8)
    XP = RG[:, 0:1036].rearrange("p (s x) -> p s x", s=4)
    SQs = [A(f"SQ{i}", [128, 512], F32) for i in range(2)]
    QNs = [A(f"QN{i}", [128, 512], F32) for i in range(2)]
    QRs = [A(f"QR{i}", [128, 512], F32) for i in range(2)]
    R12 = A("R12", [128, 1024], F32)
    R1s = [R12[:, 0:256], R12[:, 512:768]]
    R2s = [R12[:, 256:512], R12[:, 768:1024]]
    CSTr = A("CSTr", [128, 1040], F32)
    CST = CSTr[:, 0:1024].bitcast(BF16).rearrange("p (c f) -> p c f", c=4)
    KCTA = A("KCTA", [128, 2, 512], BF16)
    VCA = A("VCA", [128, 4, 2, 65], BF16)
    HF = A("HF", [128, D], F32)
    HBr = A("HBr", [128, D], F32)
    HB = [HBr[:, 0:512].bitcast(BF16), HBr[:, 512:1024].bitcast(BF16)]
    QB = [HBr[:, i * 256:(i + 1) * 256].bitcast(BF16) for i in range(4)]
    GST = [HF[:, 0:960].rearrange("p (a q) -> p a q", a=15), HBr[:, 0:960].rearrange("p (a q) -> p a q", a=15)]
    SMALL = A("SMALL", [128, 96], F32)
    EPST = A("EPST", [128, 1], F32)
    ONE = A("ONE", [128, 1], F32)
    ident = A("identS", [128, 128], BF16)
    GQK = A("GQK", [128, 4, 2, 64], F32)
    ESINK = A("ESINK", [128, 2, 8], F32)
    ROPEC = A("ROPEC", [128, 8, 32], F32)
    ROPES = A("ROPES", [128, 8, 32], F32)
    WMASK = A("WMASK", [128, 2, 256], BF16)
    COLM = A("COLM", [128, 64], F32)
    RMASK = A("RMASK", [128, 2, 128], BF16)
    CW = A("CW", [128, 2, 4, 4], F32)
    CB = A("CB", [128, 2, 4], F32)
    BA = A("BA", [128, 2, 2, 4], F32)
    BX = A("BX", [128, 2, 2, 4], F32)
    SA = A("SA", [128, 2, 2, 4], F32)
    SA2 = A("SA2", [128, 2, 2, 4], F32)
    SAh = A("SAh", [128, 2, 2, 4], F32)
    BAh = A("BAh", [128, 2, 2, 4], F32)
    BXh = A("BXh", [128, 2, 2, 4], F32)
    H0 = A("H0", [128, 2, 2, 4], F32)
    STO = A("STO", [128, 4, 2, 2, 4], F32)
    BD = A("BD", [128, 2, 2, 4, 128], BF16)
    CT = A("CT", [128, 2, 8], F32)
    CTb = A("CTb", [128, 2, 8], BF16)
    ROW = [SQs[0][0:1, :], QNs[0][0:1, :]]
    BROW = [QRs[0][0:1, :], SQs[1][0:1, :]]
    GROW = [QNs[1][0:1, :], HF[0:1, 0:512]]
    XPr = [RG, CSTr]
    XCs = [XC, HF[:, :]]
    XCbs = [XCb, R12[:, 0:512].bitcast(BF16)]
    SGs = [SG, HBr[:, :]]
    al = kb.alias
    al["XP0"] = ["RG"]; al["XP1"] = ["CST"]
    al["XC0"] = ["ptb0", "ptb1"]; al["XC1"] = ["HF"]
    al["XCb0"] = ["ptb1"]; al["XCb1"] = ["R10", "R20"]
    al["SG0"] = ["RF"]; al["SG1"] = ["QBa0", "QBa1", "QBa2", "QBa3"]
    al["R_QT"] = ["RA0", "RA1"]; al["L00"] = ["RA0"]; al["L01"] = ["RA1"]
    al["R_KT"] = ["RB0", "RB1"]; al["L10"] = ["RB0"]; al["L11"] = ["RB1"]
    al["R_VA"] = ["RC0", "RC1"]; al["LH0"] = ["RC0"]; al["LH1"] = ["RC1"]
    for t in range(NT):
        al[f"R_GT{t}"] = [f"RD{t}"]
    al["LA20"] = [f"RD{t}" for t in range(4)]; al["LA21"] = [f"RD{t}" for t in range(4, 8)]
    al["MODB_SH"] = [f"RD{t}" for t in range(4)]; al["MODB_GS"] = [f"RD{t}" for t in range(4, 8)]
    al["XC"] = ["ptb0", "ptb1"]; al["XCb"] = ["ptb1"]
    al["KNO0"] = ["RF"]; al["KNO1"] = ["RF"]; al["R_KCT"] = ["RF"]; al["SG"] = ["RF"]
    al["VO0"] = ["RG"]; al["VO1"] = ["RG"]; al["R_VC"] = ["RG"]; al["XP"] = ["RG"]
    al["yT2"] = ["RH0"]; al["WBR0"] = ["RH1"]; al["WBR1"] = ["RH2"]; al["WBR2"] = ["RH3"]; al["G"] = ["RH0", "RH1", "RH2"]
    al["GST0"] = ["HF"]; al["GST1"] = ["QBa0", "QBa1", "QBa2", "QBa3"]
    al["HB0"] = ["QBa0", "QBa1"]; al["HB1"] = ["QBa2", "QBa3"]
    for i in range(4):
        al[f"QB{i}"] = [f"QBa{i}"]
    al["row0"] = ["SQ0"]; al["row1"] = ["QN0"]; al["brow0"] = ["QR0"]; al["brow1"] = ["SQ1"]; al["grow0"] = ["QN1"]; al["grow1"] = ["HF"]
    if needed is not None:
        print("sbuf bytes remaining/partition:", nc.sbuf_bytes_remaining)

    PS = nc.alloc_psum_tensor("PS", [128, 8, 512], F32)

    def bank(i):
        return PS[:, i, :]

    def bank_bf(i):
        return PS[:, i, :].bitcast(BF16)

    def bk(i):
        return f"ps{i}"

    zT = [QT, KT]

    class WS:
        def __init__(self):
            self.items = []
            self.issued = 0

        def add(self, src_ap, view):
            self.items.append((src_ap, view))
            return len(self.items) - 1

        def slot_ap(self, i):
            src, view = self.items[i]
            s = RING[i % NR]
            if view[0] == "k8":
                n = view[1]
                return s[:, 0:8 * n].rearrange("p (k c) -> p k c", k=8)
            raise AssertionError

        def key(self, i):
            return f"ring{i % NR}"

        def acquire(self, i, base=None):
            b0 = i if base is None else base
            while self.issued <= min(b0 + NR - 1, len(self.items) - 1):
                j = self.issued
                src, view = self.items[j]
                dma("pool", self.slot_ap(j), src.rearrange("(k p) c -> p k c", p=128), writes=[self.key(j)])
                self.issued += 1
            return self.slot_ap(i), self.key(i)

    ws = WS()

    with nc.allow_non_contiguous_dma(reason="small parameter / layout loads"):
        dma("sp", ident[:], d_ident, writes=["ident"])
        dma("sp", ROPEC[:], d_ropec, writes=["consts"])
        dma("sp", ROPES[:], d_ropes, writes=["consts"])
        dma("sp", WMASK[:], d_wmask, writes=["consts"])
        dma("sp", COLM[:], d_colm, writes=["consts"])
        dma("sp", RMASK[:], d_rmask, writes=["consts"])
        dma("sp", GQK[:].rearrange("p a l d -> p (a l d)"), dram_bcast(d_gqk.rearrange("a l d -> (a l d)")), writes=["consts"])
        dma("sp", ESINK[:].rearrange("p l h -> p (l h)"), dram_bcast(d_sink.rearrange("l h -> (l h)")), writes=["esink"])
        dma("sp", CT[:], d_cond.rearrange("r (k p) -> p r k", p=128), writes=["CT"])
        for l in range(2):
            for j in range(4):
                dma("sp", CW[:, l, j, :], d_convw[l, j].rearrange("(c p) -> p c", p=128), writes=["lrup"])
            dma("sp", CB[:, l, :], d_convb[l].rearrange("(c p) -> p c", p=128), writes=["lrup"])
            for dr in range(2):
                dma("sp", BA[:, l, dr, :], d_ba[l, dr].rearrange("(c p) -> p c", p=128), writes=["lrup"])
                dma("sp", BX[:, l, dr, :], d_bx[l, dr].rearrange("(c p) -> p c", p=128), writes=["lrup"])
                dma("sp", SA[:, l, dr, :], d_lam[l, dr].rearrange("(c p) -> p c", p=128), writes=["lrup"])
                dma("sp", H0[:, l, dr, :], d_st[l, dr].rearrange("(c p) -> p c", p=128), writes=["lrup"])
        op("dve", lambda e: e.memset(EPST[:], EPS), writes=["consts"])
        op("dve", lambda e: e.memset(ONE[:], 1.0), writes=["consts"])
        op("dve", lambda e: e.memset(RG[:, :], 0.0), writes=["XP"])
        op("dve", lambda e: e.memset(RC[:, :].bitcast(BF16), 1.0), writes=["R_VA"])
        op("dve", lambda e: e.memset(VCA[:].rearrange("p a b c -> p (a b c)"), 1.0), writes=["VCA"])
        op("pool", lambda e: e.memset(BD[:].rearrange("p a b c d -> p (a b c d)"), 0.0), writes=["BD"])
        op("act", lambda e: e.activation(out=ESINK[:], in_=ESINK[:], func=AF.Exp), reads=["esink"], writes=["esink"])
        op("act", lambda e: e.activation(out=SA[:], in_=SA[:], func=AF.Exp, scale=-1.0), reads=["lrup"], writes=["lrup"])
        op("act", lambda e: e.activation(out=SA[:], in_=SA[:], func=AF.Ln, bias=ONE[:]), reads=["lrup", "consts"], writes=["lrup"])
        op("dve", lambda e: e.tensor_scalar(out=SA2[:], in0=SA[:], scalar1=-16.0, scalar2=None, op0=ALU.mult), reads=["lrup"], writes=["lrup2"])
        op("dve", lambda e: e.tensor_scalar(out=SA[:], in0=SA[:], scalar1=-8.0, scalar2=None, op0=ALU.mult), reads=["lrup", "lrup2"], writes=["lrup"])
        op("dve", lambda e: e.tensor_scalar(out=SAh[:], in0=SA[:], scalar1=0.5, scalar2=None, op0=ALU.mult), reads=["lrup"], writes=["lrup3"])
        op("dve", lambda e: e.tensor_scalar(out=BAh[:], in0=BA[:], scalar1=0.5, scalar2=None, op0=ALU.mult), reads=["lrup"], writes=["lrup3"])
        op("dve", lambda e: e.tensor_scalar(out=BXh[:], in0=BX[:], scalar1=0.5, scalar2=None, op0=ALU.mult), reads=["lrup"], writes=["lrup3"])
        op("act", lambda e: e.activation(out=CTb[:], in_=CT[:], func=AF.Silu), reads=["CT"], writes=["CTb"])

        ada_items = {}
        pass_items = []
        cols1 = {"Aq": (0, 512), "Akv": (512, 256), "Ag": (768, 512),
                 "Bq": (1280, 512), "Bk": (1792, 512), "Bv": (2304, 512), "Bg": (2816, 512)}
        cols2 = {"Cx": (3328, 512), "Cg": (3840, 512)}
        for cg in range(6):
            ada_items[(0, cg)] = ws.add(d_wada[0][:, cg * 512:(cg + 1) * 512], ("k8", 512))
        first = True
        for hv in halves:
            for l in range(2):
                it = {}
                for name, (c0, n_) in cols1.items():
                    it[name] = ws.add(d_win[l][:, c0:c0 + n_], ("k8", n_))
                if first:
                    for cg in range(6):
                        ada_items[(1, cg)] = ws.add(d_wada[1][:, cg * 512:(cg + 1) * 512], ("k8", 512))
                    first = False
                for name, (c0, n_) in cols2.items():
                    it[name] = ws.add(d_win[l][:, c0:c0 + n_], ("k8", n_))
                for ocg in range(2):
                    for b in range(3):
                        c0 = 4352 + b * 1024 + ocg * 512
                        it[("G", ocg, b)] = ws.add(d_win[l][:, c0:c0 + 512], ("k8", 512))
                for cg in range(2):
                    it[("O", cg)] = ws.add(d_wout[l][:, cg * 512:(cg + 1) * 512], ("k8", 512))
                pass_items.append(it)

    ada_n = [0]

    def ada_rows(l):
        for cg in range(6):
            slot, skey = ws.acquire(ada_items[(l, cg)])
            n = ada_n[0]
            rb = (n // 2) % 2
            dma("sp", BROW[rb][0:1, :], d_bada[l:l + 1, cg * 512:(cg + 1) * 512], writes=[f"brow{rb}"])
            if cg in (2, 3):
                dma("sp", GROW[rb][0:1, :], d_normg[l:l + 1, (cg - 2) * 512:(cg - 1) * 512], writes=[f"grow{rb}"])
            for r in range(2):
                n = ada_n[0]
                bi = n % 2
                for kc in range(8):
                    op("pe", lambda e, kc=kc, r=r, bi=bi, slot=slot: e.matmul(
                        bank(bi)[0:1, :], lhsT=CTb[:, r, kc:kc + 1], rhs=slot[:, kc, :], start=(kc == 0), stop=(kc == 7)),
                       reads=["CTb", skey], writes=[bk(bi)])
                ro = n % 2
                op("dve", lambda e, bi=bi, ro=ro, rb=rb: e.tensor_tensor(out=ROW[ro][0:1, :], in0=bank(bi)[0:1, :], in1=BROW[rb][0:1, :], op=ALU.add),
                   reads=[bk(bi), f"brow{rb}"], writes=[f"row{ro}"])
                if cg in (2, 3):
                    op("dve", lambda e, ro=ro, rb=rb: e.scalar_tensor_tensor(out=ROW[ro][0:1, :], in0=ROW[ro][0:1, :], scalar=1.0,
                                                                             in1=GROW[rb][0:1, :], op0=ALU.add, op1=ALU.mult),
                       reads=[f"row{ro}", f"grow{rb}"], writes=[f"row{ro}"])
                dma("sp", d_modd[l, r:r + 1, cg * 512:(cg + 1) * 512], ROW[ro][0:1, :], reads=[f"row{ro}"], writes=["modd"])
                ada_n[0] += 1

    with nc.allow_non_contiguous_dma(reason="small parameter / layout loads"):
        ada_rows(0)

    SS = SMALL[:, 0:8]
    RT = SMALL[:, 8:16]
    RSTD = SMALL[:, 16:24]
    SSQs = [SMALL[:, 24:32], SMALL[:, 56:64]]
    RTQs = [SMALL[:, 32:40], SMALL[:, 64:72]]
    RSQs = [SMALL[:, 40:48], SMALL[:, 72:80]]
    DENs = [SMALL[:, 48:52], SMALL[:, 80:84]]
    RDENs = [SMALL[:, 52:56], SMALL[:, 84:88]]

    cnt = {"pa": 0, "pt": 0, "pss": 0, "ptb": 0, "qb": 0, "kno": 0, "vo": 0, "ev": 0, "den": 0, "pj": 0}

    def next_pa():
        cnt["pa"] += 1
        return cnt["pa"] % 2

    PJ_BANKS = [0, 1, 4, 5, 6, 7]

    def next_pj():
        cnt["pj"] += 1
        return PJ_BANKS[cnt["pj"] % len(PJ_BANKS)]

    def next_ev():
        cnt["ev"] += 1
        return cnt["ev"] % 2

    def next_pt():
        cnt["pt"] += 1
        return 2 + cnt["pt"] % 2

    def v3(ap, h):
        return ap.rearrange("p (h d) -> p h d", h=h)

    def qknorm(src_ps, nh, gidx, l, out_ap, skey, okeys, ev):
        w = nh * 64
        SQ, QN, SSQ, RTQ, RSQ = SQs[ev], QNs[ev], SSQs[ev], RTQs[ev], RSQs[ev]
        op("act", lambda e: e.activation(out=SQ[:, 0:w], in_=src_ps, func=AF.Square), reads=[skey], writes=[f"SQ{ev}"])
        op("dve", lambda e: e.tensor_reduce(out=SSQ[:, 0:nh], in_=v3(SQ[:, 0:w], nh), axis=AX.X, op=ALU.add), reads=[f"SQ{ev}"], writes=[f"SSQ{ev}"])
        op("act", lambda e: e.activation(out=RTQ[:, 0:nh], in_=SSQ[:, 0:nh], func=AF.Sqrt, scale=1.0 / 64, bias=EPST[:]),
           reads=[f"SSQ{ev}", "consts"], writes=[f"RTQ{ev}"])
        op("dve", lambda e: e.reciprocal(out=RSQ[:, 0:nh], in_=RTQ[:, 0:nh]), reads=[f"RTQ{ev}"], writes=[f"RSQ{ev}"])
        op("dve", lambda e: e.tensor_tensor(out=v3(QN[:, 0:w], nh), in0=v3(src_ps, nh), in1=bc(RSQ[:, 0:nh], 2, 64), op=ALU.mult),
           reads=[skey, f"RSQ{ev}"], writes=[f"QN{ev}"])
        op("pool", lambda e: e.tensor_tensor(out=v3(out_ap, nh), in0=v3(QN[:, 0:w], nh), in1=bc(GQK[:, gidx, l, :], 1, nh), op=ALU.mult),
           reads=[f"QN{ev}", "consts"], writes=okeys)

    def rope(src_ap, nh, t, out_ap, skeys, okeys, ev):
        R1, R2 = R1s[ev], R2s[ev]

        def v5(ap):
            return ap.rearrange("p (h a b d) -> p h a b d", h=nh, a=2, b=2)
        x1 = v5(src_ap)[:, :, :, 0, :]
        x2 = v5(src_ap)[:, :, :, 1, :]
        o1 = v5(out_ap)[:, :, :, 0, :]
        o2 = v5(out_ap)[:, :, :, 1, :]
        cos = bc(ROPEC[:, t, :].rearrange("p (a d) -> p a d", a=2), 1, nh)
        sin = bc(ROPES[:, t, :].rearrange("p (a d) -> p a d", a=2), 1, nh)
        w = nh * 32

        def r4(ap):
            return ap[:, 0:w].rearrange("p (h a d) -> p h a d", h=nh, a=2)
        k1, k2 = f"R1{ev}", f"R2{ev}"
        op("dve", lambda e: e.tensor_tensor(out=r4(R1), in0=x1, in1=cos, op=ALU.mult), reads=skeys + ["consts"], writes=[k1])
        op("pool", lambda e: e.tensor_tensor(out=r4(R2), in0=x2, in1=sin, op=ALU.mult), reads=skeys + ["consts"], writes=[k2])
        op("dve", lambda e: e.tensor_tensor(out=o1, in0=r4(R1), in1=r4(R2), op=ALU.subtract), reads=[k1, k2], writes=okeys)
        op("pool", lambda e: e.tensor_tensor(out=r4(R1), in0=x2, in1=cos, op=ALU.mult), reads=skeys + ["consts"], writes=[k1])
        op("dve", lambda e: e.tensor_tensor(out=r4(R2), in0=x1, in1=sin, op=ALU.mult), reads=skeys + ["consts"], writes=[k2])
        op("pool", lambda e: e.tensor_tensor(out=o2, in0=r4(R1), in1=r4(R2), op=ALU.add), reads=[k1, k2], writes=okeys)

    def transposes(in_aps, dst_ap, rkeys, wkeys):
        pb = next_pt()
        pv = bank_bf(pb).rearrange("p (a b) -> p a b", a=8)
        for j, a in enumerate(in_aps):
            op("pe", lambda e, j=j, a=a: e.transpose(out=pv[:, j, :], in_=a, identity=ident[:]), reads=rkeys + ["ident"], writes=[bk(pb)])
        nn = len(in_aps)
        op("act", lambda e: e.copy(out=dst_ap, in_=pv[:, 0:nn, :]), reads=[bk(pb)], writes=wkeys)

    PEND = []
    SKEW = 3
    DEFER = []

    def flush_pend():
        while PEND:
            PEND.pop(0)()

    def proj_tok(item, ncols, evac):
        slot, skey = ws.acquire(item)
        for t in range(NT):
            pb = next_pj()
            for kc in range(8):
                op("pe", lambda e, kc=kc, t=t, pb=pb: e.matmul(bank(pb)[:, 0:ncols], lhsT=hT[:, kc, t * 128:(t + 1) * 128],
                                                               rhs=slot[:, kc, :], start=(kc == 0), stop=(kc == 7)),
                   reads=["hT", skey], writes=[bk(pb)])
            if len(PEND) >= SKEW:
                PEND.pop(0)()
            post = evac(t, pb)
            if post is not None:
                PEND.append(post)

    def proj_feat(item, evac):
        slot, skey = ws.acquire(item)
        for jc in range(4):
            for th in range(2):
                pb = next_pa()
                for kc in range(8):
                    op("pe", lambda e, kc=kc, jc=jc, th=th, pb=pb: e.matmul(bank(pb), lhsT=slot[:, kc, jc * 128:(jc + 1) * 128],
                                                                            rhs=hT[:, kc, th * 512:(th + 1) * 512], start=(kc == 0), stop=(kc == 7)),
                       reads=["hT", skey], writes=[bk(pb)])
                evac(jc, th, pb)

    def attend(units, l, between=None):
        def emit_scores(u):
            ncols = u["ncols"]
            buf = cnt["ptb"] % 2
            cnt["ptb"] += 1
            u["buf"] = buf
            order = []
            banks = u["banks"]
            for gb0 in range(0, len(banks), 2):
                grp = banks[gb0:gb0 + 2]
                pp = cnt["pss"] % 2
                cnt["pss"] += 1
                b_lo = 4 + 2 * pp
                region = PS[:, b_lo:b_lo + 2, :].rearrange("p a b -> p (a b)")
                rk = [bk(b_lo), bk(b_lo + 1)]
                rkb = u["rkeys"] + ["ident", "consts", "G"]
                for bi, bkd in enumerate(grp):
                    nbz = len(bkd["biased"])
                    for k, ch in enumerate(bkd["plain"]):
                        c0 = bi * 512 + (nbz + k) * ncols
                        op("pe", lambda e, c0=c0, ch=ch: e.matmul(region[:, c0:c0 + ncols], lhsT=ch["kT"], rhs=u["q"], start=True, stop=True),
                           reads=u["rkeys"], writes=rk)
                for bi, bkd in enumerate(grp):
                    nbz = len(bkd["biased"])
                    if nbz:
                        op("pe", lambda e, bi=bi, nbz=nbz, bkd=bkd: e.matmul(region[:, bi * 512:bi * 512 + nbz * ncols], lhsT=ident[:], rhs=bkd["bias"],
                                                                             start=True, stop=False), reads=rkb, writes=rk)
                for bi, bkd in enumerate(grp):
                    posts = [k for (k, _) in bkd.get("post", [])]
                    for k, ch in enumerate(bkd["biased"]):
                        c0 = bi * 512 + k * ncols
                        op("pe", lambda e, c0=c0, ch=ch, k=k: e.matmul(region[:, c0:c0 + ncols], lhsT=ch["kT"], rhs=u["q"], start=False, stop=(k not in posts)),
                           reads=u["rkeys"], writes=rk)
                for bi, bkd in enumerate(grp):
                    for (k, xr) in bkd.get("post", []):
                        c0 = bi * 512 + k * ncols
                        op("pe", lambda e, c0=c0, xr=xr: e.matmul(region[:, c0:c0 + ncols], lhsT=ident[:], rhs=xr, start=False, stop=True),
                           reads=rkb, writes=rk)
                ext = 0
                for bi, bkd in enumerate(grp):
                    n_here = len(bkd["biased"]) + len(bkd["plain"])
                    for k, ch in enumerate(bkd["biased"] + bkd["plain"]):
                        order.append((ch, gb0 * 512 + bi * 512 + k * ncols))
                    ext = bi * 512 + n_here * ncols
                op("act", lambda e, gb0=gb0, ext=ext, region=region: e.activation(
                    out=PTB[buf][:, gb0 * 512:gb0 * 512 + ext], in_=region[:, 0:ext], func=AF.Exp, scale=0.125),
                   reads=rk, writes=[f"ptb{buf}"])
            u["order"] = order

        def emit_pv(u):
            ncols = u["ncols"]
            buf = u["buf"]
            pb = next_pa()
            ns = len(u["slots"])
            order = u["order"]
            nchunks = len(order)
            for si, sl in enumerate(u["slots"]):
                for ci, (ch, pc) in enumerate(order):
                    op("pe", lambda e, si=si, sl=sl, ci=ci, ch=ch, pc=pc: e.matmul(
                        bank(pb)[:, si * 65:(si + 1) * 65], lhsT=PTB[buf][:, pc + sl["col"]:pc + sl["col"] + 128],
                        rhs=ch["v"], start=(ci == 0), stop=(ci == nchunks - 1)),
                       reads=[f"ptb{buf}"] + u["rkeys"], writes=[bk(pb)])
            pvv = bank(pb)[:, 0:ns * 65].rearrange("p (s c) -> p s c", s=ns)
            dn = cnt["den"] % 2
            cnt["den"] += 1
            DEN, RDEN = DENs[dn], RDENs[dn]
            if u.get("esink") is not None:
                es = u["esink"]
                if len(es.shape) == 3:
                    a_, b_ = es.shape[1], es.shape[2]
                    den_o = DEN[:, 0:ns].rearrange("p (a b) -> p a b", a=a_)
                    den_i = bank(pb)[:, 0:ns * 65].rearrange("p (a b c) -> p a b c", a=a_, b=b_)[:, :, :, 64]
                else:
                    den_o = DEN[:, 0:ns]
                    den_i = pvv[:, :, 64]
                op("dve", lambda e: e.tensor_tensor(out=den_o, in0=den_i, in1=es, op=ALU.add),
                   reads=[bk(pb), "esink"], writes=[f"DEN{dn}"])
                op("dve", lambda e: e.reciprocal(out=RDEN[:, 0:ns], in_=DEN[:, 0:ns]), reads=[f"DEN{dn}"], writes=[f"RDEN{dn}"])
            else:
                op("dve", lambda e: e.reciprocal(out=RDEN[:, 0:ns], in_=pvv[:, :, 64]), reads=[bk(pb)], writes=[f"RDEN{dn}"])
            for si, sl in enumerate(u["slots"]):
                op("dve", lambda e, si=si, sl=sl: e.scalar_tensor_tensor(out=sl["y"], in0=pvv[:, si, 0:64], scalar=RDEN[:, si:si + 1],
                                                                         in1=sl["y"], op0=ALU.mult, op1=ALU.mult),
                   reads=[bk(pb), f"RDEN{dn}"] + sl["ykeys"], writes=sl["ykeys"])
            if u.get("after") is not None:
                u["after"]()

        prev = None
        for ui, u in enumerate(units):
            emit_scores(u)
            if prev is not None:
                emit_pv(prev)
            prev = u
            if between is not None and ui in between:
                between[ui]()
        if prev is not None:
            emit_pv(prev)

    def pack_banks(ncols, biased, bias_fn, plain, posts=None):
        per = 512 // ncols
        banks = []
        k0 = 0
        while k0 < len(biased):
            n = min(per, len(biased) - k0)
            pst = [(k - k0, ap) for (k, ap) in (posts or []) if k0 <= k < k0 + n]
            banks.append(dict(biased=biased[k0:k0 + n], bias=bias_fn(k0, n), post=pst, plain=[]))
            k0 += n
        rest = list(plain)
        if banks:
            room = per - len(banks[-1]["biased"])
            banks[-1]["plain"] = rest[:room]
            rest = rest[room:]
        while rest:
            banks.append(dict(biased=[], bias=None, post=[], plain=rest[:per]))
            rest = rest[per:]
        return banks

    def y_transposes(t, b):
        transposes([GT[:, t, j * 128:(j + 1) * 128] for j in range(4)], yT[b][:, :, t * 128:(t + 1) * 128],
                   [f"R_GT{t}"], [f"yT{b}"])

    def layer_pass(hv, l, items):
        isP = (hv == "P")
        r = 0 if isP else 1
        dma("sp", MODB[:, :], dram_bcast(d_modd[l, r, 0:2 * D]), reads=["modd"], writes=["MODB_SH", "MODB_GS"])
        SH = MODB[:, 0:D]
        GS = MODB[:, D:2 * D]
        GATEB = MODB[:, 0:D]
        op("pool", lambda e: e.memset(RC[:, :].bitcast(BF16), 1.0), writes=["R_VA"])
        for ty, dw in enumerate((d_wa, d_wx)):
            for dr in range(2):
                for par in range(2):
                    src = dw[l, dr].rearrange("(c two) k j -> two k c j", two=2)[par]
                    dma("pool", BD[par * 64:(par + 1) * 64, ty, dr, :, par * 64:(par + 1) * 64], src, writes=["BD"])
        if not isP:
            dma("pool", CST[:, :, 0:128], d_cka[l].rearrange("(c p) f -> p c f", p=128), writes=["CST"])
            op("act", lambda e: e.copy(out=CST[:, :, 128:192], in_=CST[:, :, 64:128]), reads=["CST"], writes=["CST"])
            op("act", lambda e: e.copy(out=CST[:, :, 192:256], in_=CST[:, :, 0:64]), reads=["CST"], writes=["CST"])
            transposes([CST[:, c, 0:128] for c in range(4)], KCTA[:, 0, :].rearrange("p (c k) -> p c k", c=4), ["CST"], ["KCTA"])
            transposes([CST[:, c, 128:256] for c in range(4)], KCTA[:, 1, :].rearrange("p (c k) -> p c k", c=4), ["CST"], ["KCTA"])
            for c in range(4):
                dma("pool", VCA[:, c, :, 0:64], d_cva[l][c * 128:(c + 1) * 128, :].rearrange("p (h d) -> p h d", h=2), writes=["VCA"])
            op("pool", lambda e: e.memset(RG[:, :].bitcast(BF16), 1.0), writes=["R_VC"])
            dma("pool", CST[:], d_ckb[l].rearrange("(c p) f -> p c f", p=128), writes=["CST"])
            for j in range(4):
                transposes([CST[:, c, j * 128:(j + 1) * 128] for c in range(4)], KCT[:, j, :].rearrange("p (c k) -> p c k", c=4),
                           ["CST"], ["R_KCT"])
            for c in range(4):
                dma("pool", VC[:, c, :, 0:64], d_cvb[l][c * 128:(c + 1) * 128, :].rearrange("p (h d) -> p h d", h=8), writes=["R_VC"])
        junk = PS[:, 4:6, :].rearrange("p a b -> p (a b)")
        for t in range(NT):
            op("act", lambda e, t=t: e.activation(out=junk, in_=X[:, t, :], func=AF.Square, accum_out=SS[:, t:t + 1]),
               reads=[f"X{t}"], writes=[bk(4), bk(5), f"SS{t}"])
            op("act", lambda e, t=t: e.activation(out=RT[:, t:t + 1], in_=SS[:, t:t + 1], func=AF.Sqrt, scale=1.0 / D, bias=EPST[:]),
               reads=[f"SS{t}", "consts"], writes=[f"RT{t}"])
            op("dve", lambda e, t=t: e.reciprocal(out=RSTD[:, t:t + 1], in_=RT[:, t:t + 1]), reads=[f"RT{t}"], writes=[f"RSTD{t}"])
        for t in range(NT):
            hb = t % 2
            op("dve", lambda e, t=t: e.scalar_tensor_tensor(out=HF[:], in0=X[:, t, :], scalar=RSTD[:, t:t + 1], in1=GS, op0=ALU.mult, op1=ALU.mult),
               reads=[f"X{t}", f"RSTD{t}", "MODB_GS"], writes=["HF"])
            op("dve", lambda e, hb=hb: e.tensor_tensor(out=HB[hb][:], in0=HF[:], in1=SH, op=ALU.add), reads=["HF", "MODB_SH"], writes=[f"HB{hb}"])
            transposes([HB[hb][:, kc * 128:(kc + 1) * 128] for kc in range(8)], hT[:, :, t * 128:(t + 1) * 128], [f"HB{hb}"], ["hT"])
        while DEFER:
            DEFER.pop(0)()

        def evac_Aq(t, pb):
            qb = cnt["qb"] % 4
            cnt["qb"] += 1
            ev = next_ev()
            if isP:
                qknorm(bank(pb), 8, 0, l, QB[qb][:], bk(pb), [f"QB{qb}"], ev)
            else:
                qknorm(bank(pb), 8, 0, l, QRs[ev][:], bk(pb), [f"QR{ev}"], ev)
                rope(QRs[ev][:], 8, t, QB[qb][:], [f"QR{ev}"], [f"QB{qb}"], ev)
            return lambda: transposes([QB[qb][:, j * 128:(j + 1) * 128] for j in range(4)], QT[:, :, t * 128:(t + 1) * 128], [f"QB{qb}"], ["R_QT"])

        def evac_Akv(t, pb):
            qb = cnt["qb"] % 4
            cnt["qb"] += 1
            s, tok0 = t // 2, (t % 2) * 128
            ev = next_ev()
            if isP:
                ko = cnt["kno"] % 2
                cnt["kno"] += 1
                qknorm(bank(pb)[:, 0:128], 2, 1, l, KNO[ko][:, 0:128], bk(pb), [f"KNO{ko}"], ev)
                dma("sp", d_nka[s, l, tok0:tok0 + 128, :], KNO[ko][:, 0:128], reads=[f"KNO{ko}"])
                op("act", lambda e: e.copy(out=QB[qb][:, 0:128], in_=KNO[ko][:, 0:128]), reads=[f"KNO{ko}"], writes=[f"QB{qb}"])
                vo = cnt["vo"] % 2
                cnt["vo"] += 1
                op("act", lambda e: e.copy(out=VO[vo][:, 0:128], in_=bank(pb)[:, 128:256]), reads=[bk(pb)], writes=[f"VO{vo}"])
                dma("sp", d_nva[s, l, tok0:tok0 + 128, :], VO[vo][:, 0:128], reads=[f"VO{vo}"])
                op("act", lambda e: e.copy(out=VA[:, t, 0:2, 0:64], in_=v3(VO[vo][:, 0:128], 2)), reads=[f"VO{vo}"], writes=["R_VA"])
            else:
                qknorm(bank(pb)[:, 0:128], 2, 1, l, QRs[ev][:, 0:128], bk(pb), [f"QR{ev}"], ev)
                rope(QRs[ev][:, 0:128], 2, t, QB[qb][:, 0:128], [f"QR{ev}"], [f"QB{qb}"], ev)
                op("act", lambda e: e.copy(out=VA[:, t, 0:2, 0:64], in_=v3(bank(pb)[:, 128:256], 2)), reads=[bk(pb)], writes=["R_VA"])
            op("act", lambda e: e.copy(out=QB[qb][:, 128:192], in_=QB[qb][:, 64:128]), reads=[f"QB{qb}"], writes=[f"QB{qb}"])
            op("act", lambda e: e.copy(out=QB[qb][:, 192:256], in_=QB[qb][:, 0:64]), reads=[f"QB{qb}"], writes=[f"QB{qb}"])
            return lambda: transposes([QB[qb][:, 0:128], QB[qb][:, 128:256]], KT[:, 0:2, t * 128:(t + 1) * 128], [f"QB{qb}"], ["R_KT"])

        def evac_g(t, pb):
            op("act", lambda e: e.activation(out=GT[:, t, :], in_=bank(pb), func=AF.Silu), reads=[bk(pb)], writes=[f"R_GT{t}"])

        def g_build(h):
            gs = h % 2
            base = 64 + (l * 8 + h) * 465
            src_top = bass.AP(d_rpb.tensor, base + 14 * 31 - 48, [[1, 64], [-31, 15], [1, 64]])
            dma("sp", GST[gs][0:64, :, :], src_top, writes=[f"GST{gs}"])
            src_bot = bass.AP(d_rpb.tensor, base + 14 * 31 - 48, [[1, 64], [-31, 14], [1, 64]])
            dma("sp", GST[gs][64:128, 1:15, :], src_bot, writes=[f"GST{gs}"])
            dma("sp", GST[gs][64:128, 0:1, :], bass.AP(d_rpb.tensor, base + 14 * 31 - 48, [[1, 64], [-31, 1], [1, 64]]), writes=[f"GST{gs}"])
            op("dve", lambda e: e.scalar_tensor_tensor(out=G[:, h, :, :], in0=GST[gs][:, :, ::-1], scalar=8.0,
                                                       in1=bc(COLM[:], 1, 15), op0=ALU.mult, op1=ALU.add),
               reads=[f"GST{gs}", "consts"], writes=["G"])

        proj_tok(items["Aq"], 512, evac_Aq)
        proj_tok(items["Akv"], 256, evac_Akv)
        proj_tok(items["Ag"], 512, evac_g)
        flush_pend()

        units = []
        if isP:
            for s in range(4):
                for g in range(2):
                    for b in range(2):
                        kv = 0 if g == b else 1
                        chunks = [dict(kT=KT[b * 64:(b + 1) * 64, kv, s * 256 + kc * 128:s * 256 + (kc + 1) * 128],
                                       v=VA[:, 2 * s + kc, g, :]) for kc in range(2)]
                        slots = []
                        for j in range(2):
                            for qt in range(2):
                                h = 4 * g + 2 * j + b
                                t = 2 * s + qt
                                slots.append(dict(col=j * 256 + qt * 128, y=GT[:, t, h * 64:(h + 1) * 64], ykeys=[f"R_GT{t}"]))
                        h0 = 4 * g + b
                        units.append(dict(ncols=512, q=QT[b * 64:(b + 1) * 64, 2 * g:2 * g + 2, s * 256:(s + 1) * 256],
                                          banks=pack_banks(512, [], None, chunks), slots=slots, esink=bc(ESINK[:, l, h0:h0 + 3:2], 2, 2),
                                          rkeys=["R_QT", "R_KT", "R_VA"],
                                          after=(lambda s=s: [y_transposes(2 * s + qt, 0) for qt in range(2)]) if (g == 1 and b == 1) else None))
        else:
            for i in range(NT):
                for g in range(2):
                    for b in range(2):
                        kv = 0 if g == b else 1
                        def lc(c):
                            return dict(kT=KT[b * 64:(b + 1) * 64, kv, c * 128:(c + 1) * 128], v=VA[:, c, g, :])
                        biased = []
                        if i - 1 >= 0:
                            biased.append(lc(i - 1))
                        if i + 1 < NT:
                            biased.append(lc(i + 1))
                        if i - 1 >= 0 and i + 1 < NT:
                            bias_ap = WMASK[:, :, :].rearrange("p a b -> p (a b)")
                        elif i - 1 >= 0:
                            bias_ap = WMASK[:, 0, :]
                        else:
                            bias_ap = WMASK[:, 1, :]
                        plain = [lc(i)] + [dict(kT=KCTA[b * 64:(b + 1) * 64, kv, c * 128:(c + 1) * 128], v=VCA[:, c, g, :]) for c in range(4)]
                        banks = pack_banks(256, biased, (lambda k0, n, bias_ap=bias_ap: bias_ap), plain)
                        h0 = 4 * g + b
                        slots = [dict(col=j * 128, y=GT[:, i, (h0 + 2 * j) * 64:(h0 + 2 * j + 1) * 64], ykeys=[f"R_GT{i}"]) for j in range(2)]
                        units.append(dict(ncols=256, q=QT[b * 64:(b + 1) * 64, 2 * g:2 * g + 2, i * 128:(i + 1) * 128],
                                          banks=banks, slots=slots, esink=ESINK[:, l, h0:h0 + 3:2],
                                          rkeys=["R_QT", "R_KT", "R_VA", "KCTA", "VCA"],
                                          after=(lambda i=i: y_transposes(i, 0)) if (g == 1 and b == 1) else None))
        attend(units, l, between=(None if isP else {4 * h + 1: (lambda h=h: g_build(h)) for h in range(8)}))
        dbg('yTA', yT[0][:, :, :], ['yT0'])
        dbg('hT', hT[:, :, :], ['hT'])

        def evac_Bq(t, pb):
            qb = cnt["qb"] % 4
            cnt["qb"] += 1
            qknorm(bank(pb), 8, 2, l, QB[qb][:], bk(pb), [f"QB{qb}"], next_ev())
            return lambda: transposes([QB[qb][:, j * 128:(j + 1) * 128] for j in range(4)], QT[:, :, t * 128:(t + 1) * 128], [f"QB{qb}"], ["R_QT"])

        def evac_Bk(t, pb):
            qb = cnt["qb"] % 4
            cnt["qb"] += 1
            s, tok0 = t // 2, (t % 2) * 128
            if isP:
                ko = cnt["kno"] % 2
                cnt["kno"] += 1
                qknorm(bank(pb), 8, 3, l, KNO[ko][:], bk(pb), [f"KNO{ko}"], next_ev())
                dma("sp", d_nkb[s, l, tok0:tok0 + 128, :], KNO[ko][:], reads=[f"KNO{ko}"])
                op("act", lambda e: e.copy(out=QB[qb][:], in_=KNO[ko][:]), reads=[f"KNO{ko}"], writes=[f"QB{qb}"])
            else:
                qknorm(bank(pb), 8, 3, l, QB[qb][:], bk(pb), [f"QB{qb}"], next_ev())
            return lambda: transposes([QB[qb][:, j * 128:(j + 1) * 128] for j in range(4)], KT[:, :, t * 128:(t + 1) * 128], [f"QB{qb}"], ["R_KT"])

        def evac_Bv(t, pb):
            s, tok0 = t // 2, (t % 2) * 128
            if isP:
                vo = cnt["vo"] % 2
                cnt["vo"] += 1
                op("act", lambda e: e.copy(out=VO[vo][:], in_=bank(pb)), reads=[bk(pb)], writes=[f"VO{vo}"])
                dma("sp", d_nvb[s, l, tok0:tok0 + 128, :], VO[vo][:], reads=[f"VO{vo}"])
                op("act", lambda e: e.copy(out=VA[:, t, :, 0:64], in_=v3(VO[vo][:], 8)), reads=[f"VO{vo}"], writes=["R_VA"])
            else:
                op("act", lambda e: e.copy(out=VA[:, t, :, 0:64], in_=v3(bank(pb), 8)), reads=[bk(pb)], writes=["R_VA"])

        proj_tok(items["Bq"], 512, evac_Bq)
        proj_tok(items["Bk"], 512, evac_Bk)
        proj_tok(items["Bv"], 512, evac_Bv)
        proj_tok(items["Bg"], 512, evac_g)
        flush_pend()

        units = []
        if isP:
            for s in range(4):
                for h in range(8):
                    j, b0 = h // 2, (h % 2) * 64
                    chunks = [dict(kT=KT[b0:b0 + 64, j, s * 256 + kc * 128:s * 256 + (kc + 1) * 128], v=VA[:, 2 * s + kc, h, :]) for kc in range(2)]
                    slots = [dict(col=qt * 128, y=GT[:, 2 * s + qt, h * 64:(h + 1) * 64], ykeys=[f"R_GT{2 * s + qt}"]) for qt in range(2)]
                    units.append(dict(ncols=256, q=QT[b0:b0 + 64, j, s * 256:(s + 1) * 256], banks=pack_banks(256, [], None, chunks), slots=slots, esink=None,
                                      rkeys=["R_QT", "R_KT", "R_VA"],
                                      after=(lambda s=s: [y_transposes(2 * s + qt, 1) for qt in range(2)]) if h == 7 else None))
        else:
            for i in range(NT):
                nbc = nb_chunks(i)
                for h in range(8):
                    j, b0 = h // 2, (h % 2) * 64
                    Gh = G[:, h, :, :].rearrange("p a b -> p (a b)")
                    loc = sorted(nbc, key=lambda ck: -ck[0])
                    biased = [dict(kT=KT[b0:b0 + 64, j, c * 128:(c + 1) * 128], v=VA[:, c, h, :]) for (c, kind) in loc]
                    offs = [(7 - 2 * (c - i)) * 64 for (c, kind) in loc]
                    posts = [(k, RMASK[:, kind, :]) for k, (c, kind) in enumerate(loc) if kind is not None]
                    plain = [dict(kT=KCT[b0:b0 + 64, j, c * 128:(c + 1) * 128], v=VC[:, c, h, :]) for c in range(4)]
                    banks = pack_banks(128, biased, (lambda k0, n, offs=offs, Gh=Gh: Gh[:, offs[k0]:offs[k0] + n * 128]), plain, posts)
                    slots = [dict(col=0, y=GT[:, i, h * 64:(h + 1) * 64], ykeys=[f"R_GT{i}"])]
                    units.append(dict(ncols=128, q=QT[b0:b0 + 64, j, i * 128:(i + 1) * 128], banks=banks, slots=slots, esink=None,
                                      rkeys=["R_QT", "R_KT", "R_VA", "R_KCT", "R_VC"],
                                      after=(lambda i=i: y_transposes(i, 1)) if h == 7 else None))
        attend(units, l)
        dbg('yTB', yT[1][:, :, :], ['yT1'])

        if ada_n[0] < 24:
            ada_rows(1)
        nseg = 4 if isP else 1
        L = T // nseg
        op("pool", lambda e: e.memset(RG[:, :], 0.0), writes=["XP0"])
        op("pool", lambda e: e.memset(CSTr[:, :], 0.0), writes=["XP1"])
        cx_slot, cx_key = ws.acquire(items["Cx"])
        cg_slot, cg_key = ws.acquire(items["Cg"], base=items["Cx"])

        def featmm(slot, skey, jc, th, pb):
            for kc in range(8):
                op("pe", lambda e, kc=kc: e.matmul(bank(pb), lhsT=slot[:, kc, jc * 128:(jc + 1) * 128], rhs=hT[:, kc, th * 512:(th + 1) * 512],
                                                   start=(kc == 0), stop=(kc == 7)), reads=["hT", skey], writes=[bk(pb)])

        def seg(ap):
            return ap.rearrange("p (s x) -> p s x", s=nseg)

        def lru_front_a(jc, fs):
            XPf = XPr[fs][:, 0:nseg * (L + 3)].rearrange("p (s x) -> p s x", s=nseg)
            XCf, XCbf = XCs[fs], XCbs[fs]
            kxp, kxc, kxb = f"XP{fs}", f"XC{fs}", f"XCb{fs}"
            for th in range(2):
                pb = next_pj()
                featmm(cx_slot, cx_key, jc, th, pb)
                if isP:
                    dst = XPf[:, 2 * th:2 * th + 2, 2:2 + L]
                    src = bank(pb).rearrange("p (s x) -> p s x", s=2)
                else:
                    dst = XPf[:, 0, 2 + th * 512:2 + (th + 1) * 512]
                    src = bank(pb)
                op("act", lambda e, dst=dst, src=src: e.copy(out=dst, in_=src), reads=[bk(pb)], writes=[kxp])
            op("dve", lambda e: e.tensor_scalar(out=seg(XCf), in0=XPf[:, :, 0:L], scalar1=CW[:, l, 0, jc:jc + 1], scalar2=CB[:, l, jc:jc + 1],
                                                op0=ALU.mult, op1=ALU.add), reads=[kxp, "lrup"], writes=[kxc])
            for j in range(1, 4):
                op("dve", lambda e, j=j: e.scalar_tensor_tensor(out=seg(XCf), in0=XPf[:, :, j:j + L], scalar=CW[:, l, j, jc:jc + 1], in1=seg(XCf),
                                                                op0=ALU.mult, op1=ALU.add), reads=[kxp, kxc, "lrup"], writes=[kxc])
            op("act", lambda e: e.copy(out=XCbf, in_=XCf), reads=[kxc], writes=[kxb])
            if fs == 0:
                dbg('XC', XCf, [kxc])

        def lru_front_b(jc, fs):
            SGf = SGs[fs]
            ksg = f"SG{fs}"
            for th in range(2):
                pb = next_pj()
                featmm(cg_slot, cg_key, jc, th, pb)
                op("act", lambda e, th=th, pb=pb: e.activation(out=SGf[:, th * 512:(th + 1) * 512], in_=bank(pb), func=AF.Silu),
                   reads=[bk(pb)], writes=[ksg])
            if fs == 0:
                dbg('SG', SGf, [ksg])

        def lru_back(jc, fs):
            XCf, XCbf, SGf = XCs[fs], XCbs[fs], SGs[fs]
            kxc, kxb, ksg = f"XC{fs}", f"XCb{fs}", f"SG{fs}"
            for dr in range(2):
                for ty, (dst, biash) in enumerate(((LR[dr], BAh), (LI[dr], BXh))):
                    for th in range(2):
                        pb = next_pj()
                        op("pe", lambda e, th=th, pb=pb, ty=ty, dr=dr: e.matmul(bank(pb), lhsT=BD[:, ty, dr, jc, :], rhs=XCbf[:, th * 512:(th + 1) * 512],
                                                                                start=True, stop=True), reads=["BD", kxb], writes=[bk(pb)])
                        op("act", lambda e, th=th, pb=pb, dst=dst, biash=biash, dr=dr: e.activation(
                            out=dst[:, th * 512:(th + 1) * 512], in_=bank(pb), func=AF.Tanh, scale=0.5, bias=biash[:, l, dr, jc:jc + 1]),
                           reads=[bk(pb), "lrup3"], writes=[f"L{ty}{dr}"])
            for dr in range(2):
                op("act", lambda e, dr=dr: e.activation(out=LA2[dr][:], in_=LR[dr][:], func=AF.Exp, scale=SA[:, l, dr, jc:jc + 1], bias=SA[:, l, dr, jc:jc + 1]),
                   reads=[f"L0{dr}", "lrup"], writes=[f"LA2{dr}"])
                op("act", lambda e, dr=dr: e.activation(out=LR[dr][:], in_=LR[dr][:], func=AF.Exp, scale=SAh[:, l, dr, jc:jc + 1], bias=SAh[:, l, dr, jc:jc + 1]),
                   reads=[f"L0{dr}", "lrup3"], writes=[f"L0{dr}"])
            for dr in range(2):
                op("act", lambda e, dr=dr: e.activation(out=LA2[dr][:], in_=LA2[dr][:], func=AF.Sqrt, scale=-1.0, bias=ONE[:]),
                   reads=[f"LA2{dr}", "consts"], writes=[f"LA2{dr}"])
            for dr in range(2):
                op("dve", lambda e, dr=dr: e.scalar_tensor_tensor(out=LI[dr][:], in0=LI[dr][:], scalar=1.0, in1=LA2[dr][:], op0=ALU.add, op1=ALU.mult),
                   reads=[f"L1{dr}", f"LA2{dr}"], writes=[f"L1{dr}"])
                op("dve", lambda e, dr=dr: e.scalar_tensor_tensor(out=LI[dr][:], in0=LI[dr][:], scalar=0.5, in1=XCf, op0=ALU.mult, op1=ALU.mult),
                   reads=[f"L1{dr}", kxc], writes=[f"L1{dr}"])
                for s_ in range(nseg):
                    a_ap = LR[dr][:, s_ * L:(s_ + 1) * L]
                    u_ap = LI[dr][:, s_ * L:(s_ + 1) * L]
                    o_ap = LH[dr][:, s_ * L:(s_ + 1) * L]
                    init = 0.0 if isP else H0[:, l, dr, jc:jc + 1]
                    if dr == 0:
                        op("dve", lambda e, a_ap=a_ap, u_ap=u_ap, o_ap=o_ap, init=init: e.tensor_tensor_scan(
                            out=o_ap, data0=a_ap, data1=u_ap, initial=init, op0=ALU.mult, op1=ALU.add),
                           reads=[f"L0{dr}", f"L1{dr}", "lrup"], writes=[f"LH{dr}"])
                    else:
                        op("dve", lambda e, a_ap=a_ap, u_ap=u_ap, o_ap=o_ap, init=init: e.tensor_tensor_scan(
                            out=o_ap[:, ::-1], data0=a_ap[:, ::-1], data1=u_ap[:, ::-1], initial=init, op0=ALU.mult, op1=ALU.add),
                           reads=[f"L0{dr}", f"L1{dr}", "lrup"], writes=[f"LH{dr}"])
                if isP:
                    col = L - 1 if dr == 0 else 0
                    op("dve", lambda e, dr=dr, col=col: e.tensor_copy(out=STO[:, :, l, dr, jc], in_=seg(LH[dr][:])[:, :, col]),
                       reads=[f"LH{dr}"], writes=["STO"])
            if fs == 0:
                dbg('A0', LR[0][:, :], ['L00'])
                dbg('U0', LI[0][:, :], ['L10'])
                dbg('HF0', LH[0][:, :], ['LH0'])
                dbg('HB0', LH[1][:, :], ['LH1'])
            op("dve", lambda e: e.tensor_tensor(out=LH[0][:], in0=LH[0][:], in1=LH[1][:], op=ALU.add), reads=["LH0", "LH1"], writes=["LH0"])
            op("dve", lambda e: e.tensor_tensor(out=yT[2][:, jc, :], in0=LH[0][:], in1=SGf, op=ALU.mult), reads=["LH0", ksg], writes=["yT2"])

        lru_front_a(0, 0)
        lru_front_b(0, 0)
        lru_front_a(1, 1)
        for jc in range(4):
            lru_back(jc, jc % 2)
            if jc + 1 < 4:
                lru_front_b(jc + 1, (jc + 1) % 2)
            if jc + 2 < 4:
                lru_front_a(jc + 2, jc % 2)

        dbg('yTC', yT[2][:, :, :], ['yT2'])
        dma("sp", MODB[:, 0:D], dram_bcast(d_modd[l, r, 2 * D:3 * D]), reads=["modd"], writes=["MODB_SH"])
        mctr = 0
        for ocg in range(2):
            gsl = [ws.acquire(items[("G", ocg, b)], base=items[("G", ocg, 0)]) for b in range(3)]
            for b in range(3):
                dma("pool", WBR[:, b, :, :], d_wbr[l, b][:, ocg * 512:(ocg + 1) * 512].rearrange("(k p) c -> p k c", p=128), writes=[f"WBR{b}"])
            for oc4 in range(4):
                oc = ocg * 4 + oc4
                for th in range(2):
                    for b in range(3):
                        pm = 2 * (mctr % 4)
                        pg = pm + 1
                        for kc in range(4):
                            op("pe", lambda e, kc=kc, b=b, pm=pm: e.matmul(bank(pm), lhsT=WBR[:, b, kc, oc4 * 128:(oc4 + 1) * 128],
                                                                          rhs=yT[b][:, kc, th * 512:(th + 1) * 512], start=(kc == 0), stop=(kc == 3)),
                               reads=[f"WBR{b}", f"yT{b}"], writes=[bk(pm)])
                        gslot, gkey = gsl[b]
                        for kc in range(8):
                            op("pe", lambda e, kc=kc, gslot=gslot, pg=pg: e.matmul(bank(pg), lhsT=gslot[:, kc, oc4 * 128:(oc4 + 1) * 128],
                                                                                  rhs=hT[:, kc, th * 512:(th + 1) * 512], start=(kc == 0), stop=(kc == 7)),
                               reads=[gkey, "hT"], writes=[bk(pg)])
                        ms = mctr % 2
                        SQ = SQs[ms]
                        op("act", lambda e, pg=pg: e.activation(out=SQ[:], in_=bank(pg), func=AF.Sigmoid), reads=[bk(pg)], writes=[f"SQ{ms}"])
                        if b == 0:
                            zs = (mctr // 3) % 2
                            QN = QNs[zs]
                            op("dve", lambda e, pm=pm: e.tensor_tensor(out=QN[:], in0=bank(pm), in1=SQ[:], op=ALU.mult), reads=[bk(pm), f"SQ{ms}"], writes=[f"QN{zs}"])
                        elif b == 1:
                            QR = QRs[ms]
                            op("dve", lambda e, pm=pm: e.tensor_tensor(out=QR[:], in0=bank(pm), in1=SQ[:], op=ALU.mult), reads=[bk(pm), f"SQ{ms}"], writes=[f"QR{ms}"])
                            op("pool", lambda e: e.tensor_tensor(out=QN[:], in0=QN[:], in1=QR[:], op=ALU.add), reads=[f"QN{zs}", f"QR{ms}"], writes=[f"QN{zs}"])
                        else:
                            QR = QRs[ms]
                            op("dve", lambda e, pm=pm: e.tensor_tensor(out=QR[:], in0=bank(pm), in1=SQ[:], op=ALU.mult), reads=[bk(pm), f"SQ{ms}"], writes=[f"QR{ms}"])
                            zk = "R_QT" if oc < 4 else "R_KT"
                            op("pool", lambda e, oc=oc, th=th: e.tensor_tensor(out=zT[oc // 4][:, oc % 4, th * 512:(th + 1) * 512], in0=QN[:], in1=QR[:], op=ALU.add),
                               reads=[f"QN{zs}", f"QR{ms}"], writes=[zk])
                        mctr += 1

        oslots = [ws.acquire(items[("O", 0)]), ws.acquire(items[("O", 1)], base=items[("O", 0)])]
        for t in range(NT):
            for cg in range(2):
                oslot, okey = oslots[cg]
                pb = next_pj()
                for kc in range(8):
                    op("pe", lambda e, kc=kc, t=t, pb=pb, oslot=oslot: e.matmul(bank(pb), lhsT=zT[kc // 4][:, kc % 4, t * 128:(t + 1) * 128], rhs=oslot[:, kc, :],
                                                                              start=(kc == 0), stop=(kc == 7)), reads=["R_QT", "R_KT", okey], writes=[bk(pb)])
                os_ = (2 * t + cg) % 2
                op("dve", lambda e, pb=pb, cg=cg, os_=os_: e.tensor_tensor(out=QNs[os_][:], in0=bank(pb), in1=GATEB[:, cg * 512:(cg + 1) * 512], op=ALU.mult),
                   reads=[bk(pb), "MODB_SH"], writes=[f"QN{os_}"])
                op("pool", lambda e, t=t, cg=cg, os_=os_: e.tensor_tensor(out=X[:, t, cg * 512:(cg + 1) * 512], in0=X[:, t, cg * 512:(cg + 1) * 512], in1=QNs[os_][:], op=ALU.add),
                   reads=[f"QN{os_}", f"X{t}"], writes=[f"X{t}"])

    with nc.allow_non_contiguous_dma(reason="small parameter / layout loads"):
        pi = 0
        for hv in halves:
            src = d_xp if hv == "P" else d_xs
            dst = d_yp if hv == "P" else d_ys
            for t in range(NT):
                dma("sp", X[:, t, :], src[t * 128:(t + 1) * 128, :], writes=[f"X{t}"])
            for l in range(2):
                layer_pass(hv, l, pass_items[pi])
                pi += 1
            for t in range(NT):
                dma("sp", dst[t * 128:(t + 1) * 128, :], X[:, t, :], reads=[f"X{t}"])
            if hv == "P":
                def emit_nst():
                    for s in range(4):
                        for l in range(2):
                            for dr in range(2):
                                dma("sp", d_nst[s, l, dr].rearrange("(c p) -> p c", p=128), STO[:, s, l, dr, :], reads=["STO"])
                DEFER.append(emit_nst)
        while DEFER:
            DEFER.pop(0)()
        kb.finish("sp")
    if needed is not None:
        print("ops", kb.nops, "waits", kb.nwaits, "per-engine", kb.cnt)
    return nc, kb


_CONSTS = None


def kernel(x_prompt, x_sample, cache_ka, cache_va, cache_kb, cache_vb, state_lru, c, c_ctx,
           norm_g, w_ada, b_ada, w_in, a_q_norm, a_k_norm, a_sink, b_q_norm, b_k_norm, b_rpb,
           lru_conv_w, lru_conv_b, lru_wa, lru_ba, lru_wx, lru_bx, lru_lambda, w_branch, w_out, _halves=HALVES):
    global _CONSTS
    f = lambda a: np.ascontiguousarray(np.asarray(a, dtype=np.float32))
    if _CONSTS is None:
        _CONSTS = host_consts()
    cst = _CONSTS
    x_prompt, x_sample = f(x_prompt), f(x_sample)
    cache_ka, cache_va, cache_kb, cache_vb = f(cache_ka), f(cache_va), f(cache_kb), f(cache_vb)
    state_lru, c, c_ctx = f(state_lru), f(c), f(c_ctx)
    rpb = f(b_rpb).reshape(-1)
    rpbpad = np.concatenate([np.zeros(64, np.float32), rpb, np.zeros(256 - 64, np.float32)])
    gqk = np.stack([f(a_q_norm), f(a_k_norm), f(b_q_norm), f(b_k_norm)], axis=0)
    shared = {
        "norm_g": f(norm_g), "w_ada": f(w_ada), "b_ada": f(b_ada), "w_in": f(w_in), "gqk": gqk, "a_sink": f(a_sink),
        "rpbpad": rpbpad, "conv_w": f(lru_conv_w), "conv_b": f(lru_conv_b), "lru_wa": f(lru_wa), "lru_ba": f(lru_ba),
        "lru_wx": f(lru_wx), "lru_bx": f(lru_bx), "lru_lam": f(lru_lambda), "w_br": f(w_branch), "w_out": f(w_out),
        "ident": cst["ident"], "ropec": cst["ropec"], "ropes": cst["ropes"], "wmask": cst["wmask"], "colm": cst["colm"],
        "rmask": cst["rmask"],
    }
    in_maps = []
    for i in range(NCORES):
        m = dict(shared)
        m["xp"] = x_prompt[4 * i:4 * i + 4].reshape(T, D)
        m["xs"] = x_sample[i].reshape(T, D)
        m["cka"] = cache_ka[i].reshape(2, 512, 128)
        m["cva"] = cache_va[i].reshape(2, 512, 128)
        m["ckb"] = cache_kb[i].reshape(2, 512, 512)
        m["cvb"] = cache_vb[i].reshape(2, 512, 512)
        m["st"] = state_lru[i]
        m["cond"] = np.stack([c_ctx, c[i]], axis=0)
        in_maps.append(m)
    nc = build_program(_halves)
    res = run_bass_kernel_spmd(nc, in_maps, core_ids=list(range(NCORES)))
    rs = res.results
    y_prompt = np.concatenate([r["yp"].reshape(4, 256, D) for r in rs], axis=0)
    y_sample = np.stack([r["ys"].reshape(T, D) for r in rs], axis=0)
    nka = np.concatenate([r["nka"].reshape(4, 2, 256, 2, 64) for r in rs], axis=0)
    nva = np.concatenate([r["nva"].reshape(4, 2, 256, 2, 64) for r in rs], axis=0)
    nkb = np.concatenate([r["nkb"].reshape(4, 2, 256, 8, 64) for r in rs], axis=0)
    nvb = np.concatenate([r["nvb"].reshape(4, 2, 256, 8, 64) for r in rs], axis=0)
    nst = np.concatenate([r["nst"].reshape(4, 2, 2, 512) for r in rs], axis=0)
    out = (y_prompt, y_sample, nka, nva, nkb, nvb, nst)
    return tuple(np.ascontiguousarray(o, dtype=np.float32) for o in out)
```

```python
import numpy as np
import ml_dtypes
import concourse.bass as bass
import concourse.mybir as mybir
from concourse.bass_utils import run_bass_kernel_spmd

F32 = mybir.dt.float32
BF16 = mybir.dt.bfloat16
ALU = mybir.AluOpType
AF = mybir.ActivationFunctionType
AX = mybir.AxisListType

NCORES = 8
D = 1024
T = 1024
NT = 8
EPS = 1e-6
NEG = -30000.0
IN_COLS = 7424
HALVES = ("P", "S")


class KB:
    def __init__(self, nc, needed=None):
        self.nc = nc
        self.needed = needed
        self.rank = None if needed is None else {e: {idx: i + 1 for i, idx in enumerate(v)} for e, v in needed.items()}
        self.rec = {e: set() for e in ("pe", "act", "dve", "pool")}
        self.eng = {"pe": nc.tensor, "act": nc.scalar, "dve": nc.vector, "pool": nc.gpsimd, "sp": nc.sync}
        self.sem = {e: nc.alloc_semaphore("s_" + e) for e in ("pe", "act", "dve", "pool")}
        self.cnt = {e: 0 for e in self.sem}
        self.waited = {}
        self.lastw = {}
        self.readers = {}
        self.dq = {}
        self.NQ = 8
        for q in ("sp", "pool"):
            self.dq[q] = {"sems": [nc.alloc_semaphore(f"d_{q}{i}") for i in range(self.NQ)], "n": 0,
                          "val": [0] * self.NQ}
        self.nwaits = 0
        self.nops = 0
        self.alias = {}

    def ex(self, keys):
        out = []
        for k in keys:
            for a in self.alias.get(k, (k,)):
                if a not in out:
                    out.append(a)
        return out

    def _deps(self, reads, writes):
        toks = []
        for r in reads:
            t = self.lastw.get(r)
            if t is not None:
                toks.append(t)
        for w in writes:
            t = self.lastw.get(w)
            if t is not None:
                toks.append(t)
            toks.extend(self.readers.get(w, []))
        return toks

    def _wait(self, e, toks):
        best = {}
        for (sem, val, src) in toks:
            if e == "pe" and src == "pe":
                continue
            k = id(sem)
            if k not in best or best[k][1] < val:
                best[k] = (sem, val, src)
        for k, (sem, val, src) in best.items():
            if self.waited.get((e, k), 0) < val:
                wv = val
                if src != "dma":
                    self.rec[src].add(val)
                    if self.rank is not None:
                        wv = self.rank[src][val]
                self.eng[e].wait_ge(sem, wv)
                self.waited[(e, k)] = val
                self.nwaits += 1

    def _commit(self, tok, reads, writes):
        for w in writes:
            self.lastw[w] = tok
            self.readers[w] = []
        for r in reads:
            if r not in writes:
                self.readers.setdefault(r, []).append(tok)

    def op(self, e, fn, reads=(), writes=()):
        reads, writes = self.ex(reads), self.ex(writes)
        self._wait(e, self._deps(reads, writes))
        ins = fn(self.eng[e])
        self.cnt[e] += 1
        self.nops += 1
        if self.rank is None or self.cnt[e] in self.rank[e]:
            ins.then_inc(self.sem[e], 1)
        tok = (self.sem[e], self.cnt[e], e)
        self._commit(tok, reads, writes)
        return tok

    def dma(self, q, out, in_, reads=(), writes=(), **kw):
        reads, writes = self.ex(reads), self.ex(writes)
        d = self.dq[q]
        j = d["n"] % self.NQ
        d["n"] += 1
        sem = d["sems"][j]
        toks = self._deps(reads, writes)
        if d["val"][j] > 0:
            toks.append((sem, d["val"][j], "dma"))
        self._wait(q, toks)
        ins = self.eng[q].dma_start(out=out, in_=in_, **kw)
        d["val"][j] += 16
        ins.then_inc(sem, 16)
        tok = (sem, d["val"][j], "dma")
        self._commit(tok, reads, writes)
        self.nops += 1
        return tok

    def finish(self, e="sp"):
        toks = list(self.lastw.values())
        for l in self.readers.values():
            toks.extend(l)
        self._wait(e, toks)


def bc(ap, pos, n):
    l = [list(x) for x in ap.ap]
    return bass.AP(ap.tensor, ap.offset, l[:pos] + [[0, n]] + l[pos:])


def dram_bcast(ap, nparts=128):
    l = [list(x) for x in ap.ap]
    return bass.AP(ap.tensor, ap.offset, [[0, nparts]] + l)


def host_consts():
    c = {}
    c["ident"] = np.eye(128, dtype=np.float32).astype(ml_dtypes.bfloat16)
    pos = (np.arange(8)[None, :] * 128 + np.arange(128)[:, None]).astype(np.float64)
    freqs = 10000.0 ** (-np.arange(16, dtype=np.float64) / 16.0)
    row = np.floor(pos / 64.0)
    col = pos - row * 64.0
    ang = np.concatenate([row[:, :, None] * freqs[None, None, :], col[:, :, None] * freqs[None, None, :]], axis=2)
    ang32 = np.concatenate([(row.astype(np.float32)[:, :, None] * freqs.astype(np.float32)[None, None, :]),
                            (col.astype(np.float32)[:, :, None] * freqs.astype(np.float32)[None, None, :])], axis=2)
    c["ropec"] = np.cos(ang32.astype(np.float64)).astype(np.float32)
    c["ropes"] = np.sin(ang32.astype(np.float64)).astype(np.float32)
    k = np.arange(128)[:, None]
    q = np.arange(128)[None, :]
    mp = np.where(q <= k, 0.0, NEG).astype(np.float32)
    mn = np.where(k <= q, 0.0, NEG).astype(np.float32)
    c["wmask"] = np.stack([np.tile(mp, (1, 2)), np.tile(mn, (1, 2))], axis=1).astype(ml_dtypes.bfloat16)
    kc = np.arange(64)[:, None]
    qc = np.arange(64)[None, :]
    cs = np.clip(qc - 8, 0, 48)
    cm = np.where((kc >= cs) & (kc < cs + 16), 0.0, NEG).astype(np.float32)
    c["colm"] = np.concatenate([cm, cm], axis=0)
    rf = np.zeros((128, 128), np.float32)
    rf[0:64, 64:128] = NEG
    rl = np.full((128, 128), NEG, np.float32)
    rl[0:64, 64:128] = 0.0
    c["rmask"] = np.stack([rf, rl], axis=1).astype(ml_dtypes.bfloat16)
    return c


def nb_chunks(i):
    rows = [2 * i, 2 * i + 1]
    rs = [min(max(r - 4, 0), 8) for r in rows]
    out = []
    for c in range(8):
        kr = [2 * c, 2 * c + 1]
        valid = [[(rs[a] <= kr[b] < rs[a] + 8) for b in range(2)] for a in range(2)]
        flat = (valid[0][0], valid[0][1], valid[1][0], valid[1][1])
        if not any(flat):
            continue
        if all(flat):
            kind = None
        elif flat == (True, True, False, True):
            kind = 0
        elif flat == (False, False, True, False):
            kind = 1
        else:
            raise AssertionError(("unexpected row mask", i, c, flat))
        out.append((c, kind))
    return out


def build_program(halves=HALVES, debug=False):
    _, kb0 = _build(halves, debug, None)
    needed = {e: sorted(v) for e, v in kb0.rec.items()}
    nc, kb1 = _build(halves, debug, needed)
    print("signalling ops per engine:", {e: len(v) for e, v in needed.items()})
    return nc


def _build(halves, debug, needed):
    nc = bass.Bass("TRN2", target_bir_lowering=False)
    kb = KB(nc, needed)
    op, dma = kb.op, kb.dma
    dbg_done = {}

    def dbg(name, ap, keys):
        if not debug or name in dbg_done:
            return
        dbg_done[name] = True
        dt_ = nc.dram_tensor("dbg_" + name, list(ap.shape), ap.dtype, kind="ExternalOutput").ap()
        dma("sp", dt_, ap, reads=keys)

    def din(name, shape, dt=F32):
        return nc.dram_tensor(name, list(shape), dt, kind="ExternalInput").ap()

    def dout(name, shape):
        return nc.dram_tensor(name, list(shape), F32, kind="ExternalOutput").ap()

    d_xp = din("xp", [T, D])
    d_xs = din("xs", [T, D])
    d_cka = din("cka", [2, 512, 128])
    d_cva = din("cva", [2, 512, 128])
    d_ckb = din("ckb", [2, 512, 512])
    d_cvb = din("cvb", [2, 512, 512])
    d_st = din("st", [2, 2, 512])
    d_cond = din("cond", [2, D])
    d_normg = din("norm_g", [2, D])
    d_wada = din("w_ada", [2, D, 3 * D])
    d_bada = din("b_ada", [2, 3 * D])
    d_win = din("w_in", [2, D, IN_COLS])
    d_gqk = din("gqk", [4, 2, 64])
    d_sink = din("a_sink", [2, 8])
    d_rpb = din("rpbpad", [2 * 8 * 15 * 31 + 256])
    d_convw = din("conv_w", [2, 4, 512])
    d_convb = din("conv_b", [2, 512])
    d_wa = din("lru_wa", [2, 2, 8, 64, 64])
    d_ba = din("lru_ba", [2, 2, 512])
    d_wx = din("lru_wx", [2, 2, 8, 64, 64])
    d_bx = din("lru_bx", [2, 2, 512])
    d_lam = din("lru_lam", [2, 2, 512])
    d_wbr = din("w_br", [2, 3, 512, D])
    d_wout = din("w_out", [2, D, D])
    d_ident = din("ident", [128, 128], BF16)
    d_ropec = din("ropec", [128, 8, 32])
    d_ropes = din("ropes", [128, 8, 32])
    d_wmask = din("wmask", [128, 2, 256], BF16)
    d_colm = din("colm", [128, 64])
    d_rmask = din("rmask", [128, 2, 128], BF16)

    d_yp = dout("yp", [T, D])
    d_ys = dout("ys", [T, D])
    d_nka = dout("nka", [4, 2, 256, 128])
    d_nva = dout("nva", [4, 2, 256, 128])
    d_nkb = dout("nkb", [4, 2, 256, 512])
    d_nvb = dout("nvb", [4, 2, 256, 512])
    d_nst = dout("nst", [4, 2, 2, 512])
    d_modd = nc.dram_tensor("modd", [2, 2, 3 * D], F32, kind="Internal").ap()

    A = nc.alloc_sbuf_tensor
    X = A("X", [128, NT, D], F32)
    hT = A("hT", [128, 8, T], BF16)
    NR = 4
    RING = [A(f"ring{i}", [128, 4096], BF16) for i in range(NR)]
    yT01 = [A(f"yT{b}", [128, 4, T], BF16) for b in range(2)]
    RH = A("RH", [128, 5120], F32)
    yT = yT01 + [RH[:, 0:2048].bitcast(BF16).rearrange("p (k t) -> p k t", k=4)]
    WBR = RH[:, 2048:5120].bitcast(BF16).rearrange("p (b k c) -> p b k c", b=3, k=4)
    G = RH[:, 0:3840].bitcast(BF16).rearrange("p (h a q) -> p h a q", h=8, a=15)
    RA = A("RA", [128, 2048], F32)
    RB = A("RB", [128, 2048], F32)
    RC = A("RC", [128, 2080], F32)
    RD = A("RD", [128, 2048], F32)
    RE = A("RE", [128, 1792], F32)
    RF = A("RF", [128, 1040], F32)
    RG = A("RG", [128, 1040], F32)
    QT = RA[:, :].bitcast(BF16).rearrange("p (k t) -> p k t", k=4)
    KT = RB[:, :].bitcast(BF16).rearrange("p (k t) -> p k t", k=4)
    VA = RC[:, :].bitcast(BF16).rearrange("p (t h d) -> p t h d", t=NT, h=8)
    GT = RD[:, :].bitcast(BF16).rearrange("p (t c) -> p t c", t=NT)
    MODB = RD
    NPT = 1792
    PTB = [RE[:, 0:896].bitcast(BF16), RE[:, 896:1792].bitcast(BF16)]
    LR = [RA[:, 0:1024], RA[:, 1024:2048]]
    LI = [RB[:, 0:1024], RB[:, 1024:2048]]
    LH = [RC[:, 0:1024], RC[:, 1024:2048]]
    LA2 = [RD[:, 0:1024], RD[:, 1024:2048]]
    XC = RE[:, 0:1024]
    XCb = RE[:, 1024:1536].bitcast(BF16)
    KNO = [RF[:, 0:512], RF[:, 512:1024]]
    KCT = RF[:, 0:1024].bitcast(BF16).rearrange("p (k t) -> p k t", k=4)
    SG = RF[:, 0:1024]
    VO = [RG[:, 0:512], RG[:, 512:1024]]
    VC = RG[:, :].bitcast(BF16).rearrange("p (c h d) -> p c h d", c=4, h=8)
    XP = RG[:, 0:1036].rearrange("p (s x) -> p s x", s=4)
    SQs = [A(f"SQ{i}", [128, 512], F32) for i in range(2)]
    QNs = [A(f"QN{i}", [128, 512], F32) for i in range(2)]
    QRs = [A(f"QR{i}", [128, 512], F32) for i in range(2)]
    R12 = A("R12", [128, 1024], F32)
    R1s = [R12[:, 0:256], R12[:, 512:768]]
    R2s = [R12[:, 256:512], R12[:, 768:1024]]
    CSTr = A("CSTr", [128, 1040], F32)
    CST = CSTr[:, 0:1024].bitcast(BF16).rearrange("p (c f) -> p c f", c=4)
    KCTA = A("KCTA", [128, 2, 512], BF16)
    VCA = A("VCA", [128, 4, 2, 65], BF16)
    HF = A("HF", [128, D], F32)
    HBr = A("HBr", [128, D], F32)
    HB = [HBr[:, 0:512].bitcast(BF16), HBr[:, 512:1024].bitcast(BF16)]
    QB = [HBr[:, i * 256:(i + 1) * 256].bitcast(BF16) for i in range(4)]
    GST = [HF[:, 0:960].rearrange("p (a q) -> p a q", a=15), HBr[:, 0:960].rearrange("p (a q) -> p a q", a=15)]
    SMALL = A("SMALL", [128, 96], F32)
    EPST = A("EPST", [128, 1], F32)
    ONE = A("ONE", [128, 1], F32)
    ident = A("identS", [128, 128], BF16)
    GQK = A("GQK", [128, 4, 2, 64], F32)
    ESINK = A("ESINK", [128, 2, 8], F32)
    ROPEC = A("ROPEC", [128, 8, 32], F32)
    ROPES = A("ROPES", [128, 8, 32], F32)
    WMASK = A("WMASK", [128, 2, 256], BF16)
    COLM = A("COLM", [128, 64], F32)
    RMASK = A("RMASK", [128, 2, 128], BF16)
    CW = A("CW", [128, 2, 4, 4], F32)
    CB = A("CB", [128, 2, 4], F32)
    BA = A("BA", [128, 2, 2, 4], F32)
    BX = A("BX", [128, 2, 2, 4], F32)
    SA = A("SA", [128, 2, 2, 4], F32)
    SA2 = A("SA2", [128, 2, 2, 4], F32)
    SAh = A("SAh", [128, 2, 2, 4], F32)
    BAh = A("BAh", [128, 2, 2, 4], F32)
    BXh = A("BXh", [128, 2, 2, 4], F32)
    H0 = A("H0", [128, 2, 2, 4], F32)
    STO = A("STO", [128, 4, 2, 2, 4], F32)
    BD = A("BD", [128, 2, 2, 4, 128], BF16)
    CT = A("CT", [128, 2, 8], F32)
    CTb = A("CTb", [128, 2, 8], BF16)
    ROW = [SQs[0][0:1, :], QNs[0][0:1, :]]
    BROW = [QRs[0][0:1, :], SQs[1][0:1, :]]
    GROW = [QNs[1][0:1, :], HF[0:1, 0:512]]
    XPr = [RG, CSTr]
    XCs = [XC, HF[:, :]]
    XCbs = [XCb, R12[:, 0:512].bitcast(BF16)]
    SGs = [SG, HBr[:, :]]
    al = kb.alias
    al["XP0"] = ["RG"]; al["XP1"] = ["CST"]
    al["XC0"] = ["ptb0", "ptb1"]; al["XC1"] = ["HF"]
    al["XCb0"] = ["ptb1"]; al["XCb1"] = ["R10", "R20"]
    al["SG0"] = ["RF"]; al["SG1"] = ["QBa0", "QBa1", "QBa2", "QBa3"]
    al["R_QT"] = ["RA0", "RA1"]; al["L00"] = ["RA0"]; al["L01"] = ["RA1"]
    al["R_KT"] = ["RB0", "RB1"]; al["L10"] = ["RB0"]; al["L11"] = ["RB1"]
    al["R_VA"] = ["RC0", "RC1"]; al["LH0"] = ["RC0"]; al["LH1"] = ["RC1"]
    for t in range(NT):
        al[f"R_GT{t}"] = [f"RD{t}"]
    al["LA20"] = [f"RD{t}" for t in range(4)]; al["LA21"] = [f"RD{t}" for t in range(4, 8)]
    al["MODB_SH"] = [f"RD{t}" for t in range(4)]; al["MODB_GS"] = [f"RD{t}" for t in range(4, 8)]
    al["XC"] = ["ptb0", "ptb1"]; al["XCb"] = ["ptb1"]
    al["KNO0"] = ["RF"]; al["KNO1"] = ["RF"]; al["R_KCT"] = ["RF"]; al["SG"] = ["RF"]
    al["VO0"] = ["RG"]; al["VO1"] = ["RG"]; al["R_VC"] = ["RG"]; al["XP"] = ["RG"]
    al["yT2"] = ["RH0"]; al["WBR0"] = ["RH1"]; al["WBR1"] = ["RH2"]; al["WBR2"] = ["RH3"]; al["G"] = ["RH0", "RH1", "RH2"]
    al["GST0"] = ["HF"]; al["GST1"] = ["QBa0", "QBa1", "QBa2", "QBa3"]
    al["HB0"] = ["QBa0", "QBa1"]; al["HB1"] = ["QBa2", "QBa3"]
    for i in range(4):
        al[f"QB{i}"] = [f"QBa{i}"]
    al["ZA0"] = ["RC0"]; al["ZA1"] = ["RC1"]; al["ZA2"] = ["RF"]; al["ZA3"] = ["RG"]
    al["row0"] = ["SQ0"]; al["row1"] = ["QN0"]; al["brow0"] = ["QR0"]; al["brow1"] = ["SQ1"]; al["grow0"] = ["QN1"]; al["grow1"] = ["HF"]
    if needed is not None:
        print("sbuf bytes remaining/partition:", nc.sbuf_bytes_remaining)

    PS = nc.alloc_psum_tensor("PS", [128, 8, 512], F32)

    def bank(i):
        return PS[:, i, :]

    def bank_bf(i):
        return PS[:, i, :].bitcast(BF16)

    def bk(i):
        return f"ps{i}"

    zT = [QT, KT]

    class WS:
        def __init__(self):
            self.items = []
            self.issued = 0

        def add(self, src_ap, view):
            self.items.append((src_ap, view))
            return len(self.items) - 1

        def slot_ap(self, i):
            src, view = self.items[i]
            s = RING[i % NR]
            if view[0] == "k8":
                n = view[1]
                return s[:, 0:8 * n].rearrange("p (k c) -> p k c", k=8)
            raise AssertionError

        def key(self, i):
            return f"ring{i % NR}"

        def acquire(self, i, base=None):
            b0 = i if base is None else base
            while self.issued <= min(b0 + NR - 1, len(self.items) - 1):
                j = self.issued
                src, view = self.items[j]
                dma("pool", self.slot_ap(j), src.rearrange("(k p) c -> p k c", p=128), writes=[self.key(j)])
                self.issued += 1
            return self.slot_ap(i), self.key(i)

    ws = WS()

    with nc.allow_non_contiguous_dma(reason="small parameter / layout loads"):
        dma("sp", ident[:], d_ident, writes=["ident"])
        dma("sp", ROPEC[:], d_ropec, writes=["consts"])
        dma("sp", ROPES[:], d_ropes, writes=["consts"])
        dma("sp", WMASK[:], d_wmask, writes=["consts"])
        dma("sp", COLM[:], d_colm, writes=["consts"])
        dma("sp", RMASK[:], d_rmask, writes=["consts"])
        dma("sp", GQK[:].rearrange("p a l d -> p (a l d)"), dram_bcast(d_gqk.rearrange("a l d -> (a l d)")), writes=["consts"])
        dma("sp", ESINK[:].rearrange("p l h -> p (l h)"), dram_bcast(d_sink.rearrange("l h -> (l h)")), writes=["esink"])
        dma("sp", CT[:], d_cond.rearrange("r (k p) -> p r k", p=128), writes=["CT"])
        for l in range(2):
            for j in range(4):
                dma("sp", CW[:, l, j, :], d_convw[l, j].rearrange("(c p) -> p c", p=128), writes=["lrup"])
            dma("sp", CB[:, l, :], d_convb[l].rearrange("(c p) -> p c", p=128), writes=["lrup"])
            for dr in range(2):
                dma("sp", BA[:, l, dr, :], d_ba[l, dr].rearrange("(c p) -> p c", p=128), writes=["lrup"])
                dma("sp", BX[:, l, dr, :], d_bx[l, dr].rearrange("(c p) -> p c", p=128), writes=["lrup"])
                dma("sp", SA[:, l, dr, :], d_lam[l, dr].rearrange("(c p) -> p c", p=128), writes=["lrup"])
                dma("sp", H0[:, l, dr, :], d_st[l, dr].rearrange("(c p) -> p c", p=128), writes=["lrup"])
        op("dve", lambda e: e.memset(EPST[:], EPS), writes=["consts"])
        op("dve", lambda e: e.memset(ONE[:], 1.0), writes=["consts"])
        op("dve", lambda e: e.memset(RG[:, :], 0.0), writes=["XP"])
        op("dve", lambda e: e.memset(RC[:, :].bitcast(BF16), 1.0), writes=["R_VA"])
        op("dve", lambda e: e.memset(VCA[:].rearrange("p a b c -> p (a b c)"), 1.0), writes=["VCA"])
        op("pool", lambda e: e.memset(BD[:].rearrange("p a b c d -> p (a b c d)"), 0.0), writes=["BD"])
        op("act", lambda e: e.activation(out=ESINK[:], in_=ESINK[:], func=AF.Exp), reads=["esink"], writes=["esink"])
        op("act", lambda e: e.activation(out=SA[:], in_=SA[:], func=AF.Exp, scale=-1.0), reads=["lrup"], writes=["lrup"])
        op("act", lambda e: e.activation(out=SA[:], in_=SA[:], func=AF.Ln, bias=ONE[:]), reads=["lrup", "consts"], writes=["lrup"])
        op("dve", lambda e: e.tensor_scalar(out=SA2[:], in0=SA[:], scalar1=-16.0, scalar2=None, op0=ALU.mult), reads=["lrup"], writes=["lrup2"])
        op("dve", lambda e: e.tensor_scalar(out=SA[:], in0=SA[:], scalar1=-8.0, scalar2=None, op0=ALU.mult), reads=["lrup", "lrup2"], writes=["lrup"])
        op("dve", lambda e: e.tensor_scalar(out=SAh[:], in0=SA[:], scalar1=0.5, scalar2=None, op0=ALU.mult), reads=["lrup"], writes=["lrup3"])
        op("dve", lambda e: e.tensor_scalar(out=BAh[:], in0=BA[:], scalar1=0.5, scalar2=None, op0=ALU.mult), reads=["lrup"], writes=["lrup3"])
        op("dve", lambda e: e.tensor_scalar(out=BXh[:], in0=BX[:], scalar1=0.5, scalar2=None, op0=ALU.mult), reads=["lrup"], writes=["lrup3"])
        op("act", lambda e: e.activation(out=CTb[:], in_=CT[:], func=AF.Silu), reads=["CT"], writes=["CTb"])

        ada_items = {}
        pass_items = []
        cols1 = {"Aq": (0, 512), "Akv": (512, 256), "Ag": (768, 512),
                 "Bq": (1280, 512), "Bk": (1792, 512), "Bv": (2304, 512), "Bg": (2816, 512)}
        cols2 = {"Cx": (3328, 512), "Cg": (3840, 512)}
        for cg in range(6):
            ada_items[(0, cg)] = ws.add(d_wada[0][:, cg * 512:(cg + 1) * 512], ("k8", 512))
        first = True
        for hv in halves:
            for l in range(2):
                it = {}
                for name, (c0, n_) in cols1.items():
                    it[name] = ws.add(d_win[l][:, c0:c0 + n_], ("k8", n_))
                if first:
                    for cg in range(6):
                        ada_items[(1, cg)] = ws.add(d_wada[1][:, cg * 512:(cg + 1) * 512], ("k8", 512))
                    first = False
                for name, (c0, n_) in cols2.items():
                    it[name] = ws.add(d_win[l][:, c0:c0 + n_], ("k8", n_))
                for ocg in range(2):
                    for b in range(3):
                        c0 = 4352 + b * 1024 + ocg * 512
                        it[("G", ocg, b)] = ws.add(d_win[l][:, c0:c0 + 512], ("k8", 512))
                for cg in range(2):
                    it[("O", cg)] = ws.add(d_wout[l][:, cg * 512:(cg + 1) * 512], ("k8", 512))
                pass_items.append(it)

    ada_n = [0]

    def ada_rows(l):
        for cg in range(6):
            slot, skey = ws.acquire(ada_items[(l, cg)])
            n = ada_n[0]
            rb = (n // 2) % 2
            dma("sp", BROW[rb][0:1, :], d_bada[l:l + 1, cg * 512:(cg + 1) * 512], writes=[f"brow{rb}"])
            if cg in (2, 3):
                dma("sp", GROW[rb][0:1, :], d_normg[l:l + 1, (cg - 2) * 512:(cg - 1) * 512], writes=[f"grow{rb}"])
            for r in range(2):
                n = ada_n[0]
                bi = n % 2
                for kc in range(8):
                    op("pe", lambda e, kc=kc, r=r, bi=bi, slot=slot: e.matmul(
                        bank(bi)[0:1, :], lhsT=CTb[:, r, kc:kc + 1], rhs=slot[:, kc, :], start=(kc == 0), stop=(kc == 7)),
                       reads=["CTb", skey], writes=[bk(bi)])
                ro = n % 2
                op("dve", lambda e, bi=bi, ro=ro, rb=rb: e.tensor_tensor(out=ROW[ro][0:1, :], in0=bank(bi)[0:1, :], in1=BROW[rb][0:1, :], op=ALU.add),
                   reads=[bk(bi), f"brow{rb}"], writes=[f"row{ro}"])
                if cg in (2, 3):
                    op("dve", lambda e, ro=ro, rb=rb: e.scalar_tensor_tensor(out=ROW[ro][0:1, :], in0=ROW[ro][0:1, :], scalar=1.0,
                                                                             in1=GROW[rb][0:1, :], op0=ALU.add, op1=ALU.mult),
                       reads=[f"row{ro}", f"grow{rb}"], writes=[f"row{ro}"])
                dma("sp", d_modd[l, r:r + 1, cg * 512:(cg + 1) * 512], ROW[ro][0:1, :], reads=[f"row{ro}"], writes=["modd"])
                ada_n[0] += 1

    with nc.allow_non_contiguous_dma(reason="small parameter / layout loads"):
        ada_rows(0)

    SS = SMALL[:, 0:8]
    RT = SMALL[:, 8:16]
    RSTD = SMALL[:, 16:24]
    SSQs = [SMALL[:, 24:32], SMALL[:, 56:64]]
    RTQs = [SMALL[:, 32:40], SMALL[:, 64:72]]
    RSQs = [SMALL[:, 40:48], SMALL[:, 72:80]]
    DENs = [SMALL[:, 48:52], SMALL[:, 80:84]]
    RDENs = [SMALL[:, 52:56], SMALL[:, 84:88]]

    cnt = {"pa": 0, "pt": 0, "pss": 0, "ptb": 0, "qb": 0, "kno": 0, "vo": 0, "ev": 0, "den": 0, "pj": 0}

    def next_pa():
        cnt["pa"] += 1
        return cnt["pa"] % 2

    PJ_BANKS = [0, 1, 4, 5, 6, 7]

    def next_pj():
        cnt["pj"] += 1
        return PJ_BANKS[cnt["pj"] % len(PJ_BANKS)]

    def next_ev():
        cnt["ev"] += 1
        return cnt["ev"] % 2

    def next_pt():
        cnt["pt"] += 1
        return 2 + cnt["pt"] % 2

    def v3(ap, h):
        return ap.rearrange("p (h d) -> p h d", h=h)

    def qknorm(src_ps, nh, gidx, l, out_ap, skey, okeys, ev, geng="pool"):
        w = nh * 64
        SQ, QN, SSQ, RTQ, RSQ = SQs[ev], QNs[ev], SSQs[ev], RTQs[ev], RSQs[ev]
        op("act", lambda e: e.activation(out=SQ[:, 0:w], in_=src_ps, func=AF.Square), reads=[skey], writes=[f"SQ{ev}"])
        op("dve", lambda e: e.tensor_reduce(out=SSQ[:, 0:nh], in_=v3(SQ[:, 0:w], nh), axis=AX.X, op=ALU.add), reads=[f"SQ{ev}"], writes=[f"SSQ{ev}"])
        op("act", lambda e: e.activation(out=RTQ[:, 0:nh], in_=SSQ[:, 0:nh], func=AF.Sqrt, scale=1.0 / 64, bias=EPST[:]),
           reads=[f"SSQ{ev}", "consts"], writes=[f"RTQ{ev}"])
        op("dve", lambda e: e.reciprocal(out=RSQ[:, 0:nh], in_=RTQ[:, 0:nh]), reads=[f"RTQ{ev}"], writes=[f"RSQ{ev}"])
        op("dve", lambda e: e.tensor_tensor(out=v3(QN[:, 0:w], nh), in0=v3(src_ps, nh), in1=bc(RSQ[:, 0:nh], 2, 64), op=ALU.mult),
           reads=[skey, f"RSQ{ev}"], writes=[f"QN{ev}"])
        op(geng, lambda e: e.tensor_tensor(out=v3(out_ap, nh), in0=v3(QN[:, 0:w], nh), in1=bc(GQK[:, gidx, l, :], 1, nh), op=ALU.mult),
           reads=[f"QN{ev}", "consts"], writes=okeys)

    def rope(src_ap, nh, t, out_ap, skeys, okeys, ev):
        R1, R2 = R1s[ev], R2s[ev]

        def v5(ap):
            return ap.rearrange("p (h a b d) -> p h a b d", h=nh, a=2, b=2)
        x1 = v5(src_ap)[:, :, :, 0, :]
        x2 = v5(src_ap)[:, :, :, 1, :]
        o1 = v5(out_ap)[:, :, :, 0, :]
        o2 = v5(out_ap)[:, :, :, 1, :]
        cos = bc(ROPEC[:, t, :].rearrange("p (a d) -> p a d", a=2), 1, nh)
        sin = bc(ROPES[:, t, :].rearrange("p (a d) -> p a d", a=2), 1, nh)
        w = nh * 32

        def r4(ap):
            return ap[:, 0:w].rearrange("p (h a d) -> p h a d", h=nh, a=2)
        k1, k2 = f"R1{ev}", f"R2{ev}"
        op("dve", lambda e: e.tensor_tensor(out=r4(R1), in0=x1, in1=cos, op=ALU.mult), reads=skeys + ["consts"], writes=[k1])
        op("pool", lambda e: e.tensor_tensor(out=r4(R2), in0=x2, in1=sin, op=ALU.mult), reads=skeys + ["consts"], writes=[k2])
        op("dve", lambda e: e.tensor_tensor(out=o1, in0=r4(R1), in1=r4(R2), op=ALU.subtract), reads=[k1, k2], writes=okeys)
        op("pool", lambda e: e.tensor_tensor(out=r4(R1), in0=x2, in1=cos, op=ALU.mult), reads=skeys + ["consts"], writes=[k1])
        op("dve", lambda e: e.tensor_tensor(out=r4(R2), in0=x1, in1=sin, op=ALU.mult), reads=skeys + ["consts"], writes=[k2])
        op("pool", lambda e: e.tensor_tensor(out=o2, in0=r4(R1), in1=r4(R2), op=ALU.add), reads=[k1, k2], writes=okeys)

    def transposes(in_aps, dst_ap, rkeys, wkeys):
        pb = next_pt()
        pv = bank_bf(pb).rearrange("p (a b) -> p a b", a=8)
        for j, a in enumerate(in_aps):
            op("pe", lambda e, j=j, a=a: e.transpose(out=pv[:, j, :], in_=a, identity=ident[:]), reads=rkeys + ["ident"], writes=[bk(pb)])
        nn = len(in_aps)
        op("act", lambda e: e.copy(out=dst_ap, in_=pv[:, 0:nn, :]), reads=[bk(pb)], writes=wkeys)

    PEND = []
    SKEW = 3
    DEFER = []

    def flush_pend():
        while PEND:
            PEND.pop(0)()

    def proj_tok(item, ncols, evac):
        slot, skey = ws.acquire(item)
        for t in range(NT):
            pb = next_pj()
            for kc in range(8):
                op("pe", lambda e, kc=kc, t=t, pb=pb: e.matmul(bank(pb)[:, 0:ncols], lhsT=hT[:, kc, t * 128:(t + 1) * 128],
                                                               rhs=slot[:, kc, :], start=(kc == 0), stop=(kc == 7)),
                   reads=["hT", skey], writes=[bk(pb)])
            if len(PEND) >= SKEW:
                PEND.pop(0)()
            post = evac(t, pb)
            if post is not None:
                PEND.append(post)

    def proj_feat(item, evac):
        slot, skey = ws.acquire(item)
        for jc in range(4):
            for th in range(2):
                pb = next_pa()
                for kc in range(8):
                    op("pe", lambda e, kc=kc, jc=jc, th=th, pb=pb: e.matmul(bank(pb), lhsT=slot[:, kc, jc * 128:(jc + 1) * 128],
                                                                            rhs=hT[:, kc, th * 512:(th + 1) * 512], start=(kc == 0), stop=(kc == 7)),
                       reads=["hT", skey], writes=[bk(pb)])
                evac(jc, th, pb)

    def attend(units, l, between=None):
        def emit_scores(u):
            ncols = u["ncols"]
            buf = cnt["ptb"] % 2
            cnt["ptb"] += 1
            u["buf"] = buf
            order = []
            banks = u["banks"]
            for gb0 in range(0, len(banks), 2):
                grp = banks[gb0:gb0 + 2]
                pp = cnt["pss"] % 2
                cnt["pss"] += 1
                b_lo = 4 + 2 * pp
                region = PS[:, b_lo:b_lo + 2, :].rearrange("p a b -> p (a b)")
                rk = [bk(b_lo), bk(b_lo + 1)]
                rkb = u["rkeys"] + ["ident", "consts", "G"]
                for bi, bkd in enumerate(grp):
                    nbz = len(bkd["biased"])
                    for k, ch in enumerate(bkd["plain"]):
                        c0 = bi * 512 + (nbz + k) * ncols
                        op("pe", lambda e, c0=c0, ch=ch: e.matmul(region[:, c0:c0 + ncols], lhsT=ch["kT"], rhs=u["q"], start=True, stop=True),
                           reads=u["rkeys"], writes=rk)
                for bi, bkd in enumerate(grp):
                    nbz = len(bkd["biased"])
                    if nbz:
                        op("pe", lambda e, bi=bi, nbz=nbz, bkd=bkd: e.matmul(region[:, bi * 512:bi * 512 + nbz * ncols], lhsT=ident[:], rhs=bkd["bias"],
                                                                             start=True, stop=False), reads=rkb, writes=rk)
                for bi, bkd in enumerate(grp):
                    npost = len(bkd.get("post", []))
                    nbz = len(bkd["biased"])
                    for k, ch in enumerate(bkd["biased"]):
                        c0 = bi * 512 + k * ncols
                        last = (k == nbz - 1) and npost == 0
                        op("pe", lambda e, c0=c0, ch=ch, last=last: e.matmul(region[:, c0:c0 + ncols], lhsT=ch["kT"], rhs=u["q"], start=False, stop=last),
                           reads=u["rkeys"], writes=rk)
                for bi, bkd in enumerate(grp):
                    posts = bkd.get("post", [])
                    for pi_, (k, xr) in enumerate(posts):
                        c0 = bi * 512 + k * ncols
                        last = (pi_ == len(posts) - 1)
                        op("pe", lambda e, c0=c0, xr=xr, last=last: e.matmul(region[:, c0:c0 + ncols], lhsT=ident[:], rhs=xr, start=False, stop=last),
                           reads=rkb, writes=rk)
                ext = 0
                for bi, bkd in enumerate(grp):
                    n_here = len(bkd["biased"]) + len(bkd["plain"])
                    for k, ch in enumerate(bkd["biased"] + bkd["plain"]):
                        order.append((ch, gb0 * 512 + bi * 512 + k * ncols))
                    ext = bi * 512 + n_here * ncols
                op("act", lambda e, gb0=gb0, ext=ext, region=region: e.activation(
                    out=PTB[buf][:, gb0 * 512:gb0 * 512 + ext], in_=region[:, 0:ext], func=AF.Exp, scale=0.125),
                   reads=rk, writes=[f"ptb{buf}"])
            u["order"] = order

        def emit_pv(u):
            ncols = u["ncols"]
            buf = u["buf"]
            pb = next_pa()
            ns = len(u["slots"])
            order = u["order"]
            nchunks = len(order)
            for si, sl in enumerate(u["slots"]):
                for ci, (ch, pc) in enumerate(order):
                    op("pe", lambda e, si=si, sl=sl, ci=ci, ch=ch, pc=pc: e.matmul(
                        bank(pb)[:, si * 65:(si + 1) * 65], lhsT=PTB[buf][:, pc + sl["col"]:pc + sl["col"] + 128],
                        rhs=ch["v"], start=(ci == 0), stop=(ci == nchunks - 1)),
                       reads=[f"ptb{buf}"] + u["rkeys"], writes=[bk(pb)])
            pvv = bank(pb)[:, 0:ns * 65].rearrange("p (s c) -> p s c", s=ns)
            dn = cnt["den"] % 2
            cnt["den"] += 1
            DEN, RDEN = DENs[dn], RDENs[dn]
            if u.get("esink") is not None:
                es = u["esink"]
                if len(es.shape) == 3:
                    a_, b_ = es.shape[1], es.shape[2]
                    den_o = DEN[:, 0:ns].rearrange("p (a b) -> p a b", a=a_)
                    den_i = bank(pb)[:, 0:ns * 65].rearrange("p (a b c) -> p a b c", a=a_, b=b_)[:, :, :, 64]
                else:
                    den_o = DEN[:, 0:ns]
                    den_i = pvv[:, :, 64]
                op("dve", lambda e: e.tensor_tensor(out=den_o, in0=den_i, in1=es, op=ALU.add),
                   reads=[bk(pb), "esink"], writes=[f"DEN{dn}"])
                op("dve", lambda e: e.reciprocal(out=RDEN[:, 0:ns], in_=DEN[:, 0:ns]), reads=[f"DEN{dn}"], writes=[f"RDEN{dn}"])
            else:
                op("dve", lambda e: e.reciprocal(out=RDEN[:, 0:ns], in_=pvv[:, :, 64]), reads=[bk(pb)], writes=[f"RDEN{dn}"])
            for si, sl in enumerate(u["slots"]):
                op("dve", lambda e, si=si, sl=sl: e.scalar_tensor_tensor(out=sl["y"], in0=pvv[:, si, 0:64], scalar=RDEN[:, si:si + 1],
                                                                         in1=sl["y"], op0=ALU.mult, op1=ALU.mult),
                   reads=[bk(pb), f"RDEN{dn}"] + sl["ykeys"], writes=sl["ykeys"])
            if u.get("after") is not None:
                u["after"]()

        prev = None
        for ui, u in enumerate(units):
            emit_scores(u)
            if prev is not None:
                emit_pv(prev)
            prev = u
            if between is not None and ui in between:
                between[ui]()
        if prev is not None:
            emit_pv(prev)

    def pack_banks(ncols, biased, bias_fn, plain, posts=None):
        per = 512 // ncols
        banks = []
        k0 = 0
        while k0 < len(biased):
            n = min(per, len(biased) - k0)
            pst = [(k - k0, ap) for (k, ap) in (posts or []) if k0 <= k < k0 + n]
            banks.append(dict(biased=biased[k0:k0 + n], bias=bias_fn(k0, n), post=pst, plain=[]))
            k0 += n
        rest = list(plain)
        if banks:
            room = per - len(banks[-1]["biased"])
            banks[-1]["plain"] = rest[:room]
            rest = rest[room:]
        while rest:
            banks.append(dict(biased=[], bias=None, post=[], plain=rest[:per]))
            rest = rest[per:]
        return banks

    def y_transposes(t, b):
        transposes([GT[:, t, j * 128:(j + 1) * 128] for j in range(4)], yT[b][:, :, t * 128:(t + 1) * 128],
                   [f"R_GT{t}"], [f"yT{b}"])

    def layer_pass(hv, l, items):
        isP = (hv == "P")
        r = 0 if isP else 1
        dma("sp", MODB[:, :], dram_bcast(d_modd[l, r, 0:2 * D]), reads=["modd"], writes=["MODB_SH", "MODB_GS"])
        SH = MODB[:, 0:D]
        GS = MODB[:, D:2 * D]
        GATEB = MODB[:, 0:D]
        op("pool", lambda e: e.memset(RC[:, :].bitcast(BF16), 1.0), writes=["R_VA"])
        for ty, dw in enumerate((d_wa, d_wx)):
            for dr in range(2):
                for par in range(2):
                    src = dw[l, dr].rearrange("(c two) k j -> two k c j", two=2)[par]
                    dma("pool", BD[par * 64:(par + 1) * 64, ty, dr, :, par * 64:(par + 1) * 64], src, writes=["BD"])
        if not isP:
            dma("pool", CST[:, :, 0:128], d_cka[l].rearrange("(c p) f -> p c f", p=128), writes=["CST"])
            op("act", lambda e: e.copy(out=CST[:, :, 128:192], in_=CST[:, :, 64:128]), reads=["CST"], writes=["CST"])
            op("act", lambda e: e.copy(out=CST[:, :, 192:256], in_=CST[:, :, 0:64]), reads=["CST"], writes=["CST"])
            transposes([CST[:, c, 0:128] for c in range(4)], KCTA[:, 0, :].rearrange("p (c k) -> p c k", c=4), ["CST"], ["KCTA"])
            transposes([CST[:, c, 128:256] for c in range(4)], KCTA[:, 1, :].rearrange("p (c k) -> p c k", c=4), ["CST"], ["KCTA"])
            for c in range(4):
                dma("pool", VCA[:, c, :, 0:64], d_cva[l][c * 128:(c + 1) * 128, :].rearrange("p (h d) -> p h d", h=2), writes=["VCA"])
            op("pool", lambda e: e.memset(RG[:, :].bitcast(BF16), 1.0), writes=["R_VC"])
            dma("pool", CST[:], d_ckb[l].rearrange("(c p) f -> p c f", p=128), writes=["CST"])
            for j in range(4):
                transposes([CST[:, c, j * 128:(j + 1) * 128] for c in range(4)], KCT[:, j, :].rearrange("p (c k) -> p c k", c=4),
                           ["CST"], ["R_KCT"])
            for c in range(4):
                dma("pool", VC[:, c, :, 0:64], d_cvb[l][c * 128:(c + 1) * 128, :].rearrange("p (h d) -> p h d", h=8), writes=["R_VC"])
        junk = PS[:, 4:6, :].rearrange("p a b -> p (a b)")
        for t in range(NT):
            op("act", lambda e, t=t: e.activation(out=junk, in_=X[:, t, :], func=AF.Square, accum_out=SS[:, t:t + 1]),
               reads=[f"X{t}"], writes=[bk(4), bk(5), f"SS{t}"])
            op("act", lambda e, t=t: e.activation(out=RT[:, t:t + 1], in_=SS[:, t:t + 1], func=AF.Sqrt, scale=1.0 / D, bias=EPST[:]),
               reads=[f"SS{t}", "consts"], writes=[f"RT{t}"])
            op("dve", lambda e, t=t: e.reciprocal(out=RSTD[:, t:t + 1], in_=RT[:, t:t + 1]), reads=[f"RT{t}"], writes=[f"RSTD{t}"])
        for t in range(NT):
            hb = t % 2
            op("dve", lambda e, t=t: e.scalar_tensor_tensor(out=HF[:], in0=X[:, t, :], scalar=RSTD[:, t:t + 1], in1=GS, op0=ALU.mult, op1=ALU.mult),
               reads=[f"X{t}", f"RSTD{t}", "MODB_GS"], writes=["HF"])
            op("dve", lambda e, hb=hb: e.tensor_tensor(out=HB[hb][:], in0=HF[:], in1=SH, op=ALU.add), reads=["HF", "MODB_SH"], writes=[f"HB{hb}"])
            transposes([HB[hb][:, kc * 128:(kc + 1) * 128] for kc in range(8)], hT[:, :, t * 128:(t + 1) * 128], [f"HB{hb}"], ["hT"])
        while DEFER:
            DEFER.pop(0)()

        def evac_Aq(t, pb):
            qb = cnt["qb"] % 4
            cnt["qb"] += 1
            ev = next_ev()
            if isP:
                qknorm(bank(pb), 8, 0, l, QB[qb][:], bk(pb), [f"QB{qb}"], ev)
            else:
                qknorm(bank(pb), 8, 0, l, QRs[ev][:], bk(pb), [f"QR{ev}"], ev, geng="dve")
                rope(QRs[ev][:], 8, t, QB[qb][:], [f"QR{ev}"], [f"QB{qb}"], ev)
            return lambda: transposes([QB[qb][:, j * 128:(j + 1) * 128] for j in range(4)], QT[:, :, t * 128:(t + 1) * 128], [f"QB{qb}"], ["R_QT"])

        def evac_Akv(t, pb):
            qb = cnt["qb"] % 4
            cnt["qb"] += 1
            s, tok0 = t // 2, (t % 2) * 128
            ev = next_ev()
            if isP:
                ko = cnt["kno"] % 2
                cnt["kno"] += 1
                qknorm(bank(pb)[:, 0:128], 2, 1, l, KNO[ko][:, 0:128], bk(pb), [f"KNO{ko}"], ev)
                dma("sp", d_nka[s, l, tok0:tok0 + 128, :], KNO[ko][:, 0:128], reads=[f"KNO{ko}"])
                op("act", lambda e: e.copy(out=QB[qb][:, 0:128], in_=KNO[ko][:, 0:128]), reads=[f"KNO{ko}"], writes=[f"QB{qb}"])
                vo = cnt["vo"] % 2
                cnt["vo"] += 1
                op("act", lambda e: e.copy(out=VO[vo][:, 0:128], in_=bank(pb)[:, 128:256]), reads=[bk(pb)], writes=[f"VO{vo}"])
                dma("sp", d_nva[s, l, tok0:tok0 + 128, :], VO[vo][:, 0:128], reads=[f"VO{vo}"])
                op("act", lambda e: e.copy(out=VA[:, t, 0:2, 0:64], in_=v3(VO[vo][:, 0:128], 2)), reads=[f"VO{vo}"], writes=["R_VA"])
            else:
                qknorm(bank(pb)[:, 0:128], 2, 1, l, QRs[ev][:, 0:128], bk(pb), [f"QR{ev}"], ev, geng="dve")
                rope(QRs[ev][:, 0:128], 2, t, QB[qb][:, 0:128], [f"QR{ev}"], [f"QB{qb}"], ev)
                op("act", lambda e: e.copy(out=VA[:, t, 0:2, 0:64], in_=v3(bank(pb)[:, 128:256], 2)), reads=[bk(pb)], writes=["R_VA"])
            op("act", lambda e: e.copy(out=QB[qb][:, 128:192], in_=QB[qb][:, 64:128]), reads=[f"QB{qb}"], writes=[f"QB{qb}"])
            op("act", lambda e: e.copy(out=QB[qb][:, 192:256], in_=QB[qb][:, 0:64]), reads=[f"QB{qb}"], writes=[f"QB{qb}"])
            return lambda: transposes([QB[qb][:, 0:128], QB[qb][:, 128:256]], KT[:, 0:2, t * 128:(t + 1) * 128], [f"QB{qb}"], ["R_KT"])

        def evac_g(t, pb):
            op("act", lambda e: e.activation(out=GT[:, t, :], in_=bank(pb), func=AF.Silu), reads=[bk(pb)], writes=[f"R_GT{t}"])

        def g_build(h):
            gs = h % 2
            base = 64 + (l * 8 + h) * 465
            src_top = bass.AP(d_rpb.tensor, base + 14 * 31 - 48, [[1, 64], [-31, 15], [1, 64]])
            dma("sp", GST[gs][0:64, :, :], src_top, writes=[f"GST{gs}"])
            src_bot = bass.AP(d_rpb.tensor, base + 14 * 31 - 48, [[1, 64], [-31, 14], [1, 64]])
            dma("sp", GST[gs][64:128, 1:15, :], src_bot, writes=[f"GST{gs}"])
            dma("sp", GST[gs][64:128, 0:1, :], bass.AP(d_rpb.tensor, base + 14 * 31 - 48, [[1, 64], [-31, 1], [1, 64]]), writes=[f"GST{gs}"])
            op("dve", lambda e: e.scalar_tensor_tensor(out=G[:, h, :, :], in0=GST[gs][:, :, ::-1], scalar=8.0,
                                                       in1=bc(COLM[:], 1, 15), op0=ALU.mult, op1=ALU.add),
               reads=[f"GST{gs}", "consts"], writes=["G"])

        proj_tok(items["Aq"], 512, evac_Aq)
        proj_tok(items["Akv"], 256, evac_Akv)
        proj_tok(items["Ag"], 512, evac_g)
        flush_pend()

        units = []
        if isP:
            for s in range(4):
                for g in range(2):
                    for b in range(2):
                        kv = 0 if g == b else 1
                        chunks = [dict(kT=KT[b * 64:(b + 1) * 64, kv, s * 256 + kc * 128:s * 256 + (kc + 1) * 128],
                                       v=VA[:, 2 * s + kc, g, :]) for kc in range(2)]
                        slots = []
                        for j in range(2):
                            for qt in range(2):
                                h = 4 * g + 2 * j + b
                                t = 2 * s + qt
                                slots.append(dict(col=j * 256 + qt * 128, y=GT[:, t, h * 64:(h + 1) * 64], ykeys=[f"R_GT{t}"]))
                        h0 = 4 * g + b
                        units.append(dict(ncols=512, q=QT[b * 64:(b + 1) * 64, 2 * g:2 * g + 2, s * 256:(s + 1) * 256],
                                          banks=pack_banks(512, [], None, chunks), slots=slots, esink=bc(ESINK[:, l, h0:h0 + 3:2], 2, 2),
                                          rkeys=["R_QT", "R_KT", "R_VA"],
                                          after=(lambda s=s: [y_transposes(2 * s + qt, 0) for qt in range(2)]) if (g == 1 and b == 1) else None))
        else:
            for i in range(NT):
                for g in range(2):
                    for b in range(2):
                        kv = 0 if g == b else 1
                        def lc(c):
                            return dict(kT=KT[b * 64:(b + 1) * 64, kv, c * 128:(c + 1) * 128], v=VA[:, c, g, :])
                        biased = []
                        if i - 1 >= 0:
                            biased.append(lc(i - 1))
                        if i + 1 < NT:
                            biased.append(lc(i + 1))
                        if i - 1 >= 0 and i + 1 < NT:
                            bias_ap = WMASK[:, :, :].rearrange("p a b -> p (a b)")
                        elif i - 1 >= 0:
                            bias_ap = WMASK[:, 0, :]
                        else:
                            bias_ap = WMASK[:, 1, :]
                        plain = [lc(i)] + [dict(kT=KCTA[b * 64:(b + 1) * 64, kv, c * 128:(c + 1) * 128], v=VCA[:, c, g, :]) for c in range(4)]
                        banks = pack_banks(256, biased, (lambda k0, n, bias_ap=bias_ap: bias_ap), plain)
                        h0 = 4 * g + b
                        slots = [dict(col=j * 128, y=GT[:, i, (h0 + 2 * j) * 64:(h0 + 2 * j + 1) * 64], ykeys=[f"R_GT{i}"]) for j in range(2)]
                        units.append(dict(ncols=256, q=QT[b * 64:(b + 1) * 64, 2 * g:2 * g + 2, i * 128:(i + 1) * 128],
                                          banks=banks, slots=slots, esink=ESINK[:, l, h0:h0 + 3:2],
                                          rkeys=["R_QT", "R_KT", "R_VA", "KCTA", "VCA"],
                                          after=(lambda i=i: y_transposes(i, 0)) if (g == 1 and b == 1) else None))
        attend(units, l, between=(None if isP else {4 * h + 1: (lambda h=h: g_build(h)) for h in range(8)}))
        dbg('yTA', yT[0][:, :, :], ['yT0'])
        dbg('hT', hT[:, :, :], ['hT'])

        def evac_Bq(t, pb):
            qb = cnt["qb"] % 4
            cnt["qb"] += 1
            qknorm(bank(pb), 8, 2, l, QB[qb][:], bk(pb), [f"QB{qb}"], next_ev())
            return lambda: transposes([QB[qb][:, j * 128:(j + 1) * 128] for j in range(4)], QT[:, :, t * 128:(t + 1) * 128], [f"QB{qb}"], ["R_QT"])

        def evac_Bk(t, pb):
            qb = cnt["qb"] % 4
            cnt["qb"] += 1
            s, tok0 = t // 2, (t % 2) * 128
            if isP:
                ko = cnt["kno"] % 2
                cnt["kno"] += 1
                qknorm(bank(pb), 8, 3, l, KNO[ko][:], bk(pb), [f"KNO{ko}"], next_ev())
                dma("sp", d_nkb[s, l, tok0:tok0 + 128, :], KNO[ko][:], reads=[f"KNO{ko}"])
                op("act", lambda e: e.copy(out=QB[qb][:], in_=KNO[ko][:]), reads=[f"KNO{ko}"], writes=[f"QB{qb}"])
            else:
                qknorm(bank(pb), 8, 3, l, QB[qb][:], bk(pb), [f"QB{qb}"], next_ev())
            return lambda: transposes([QB[qb][:, j * 128:(j + 1) * 128] for j in range(4)], KT[:, :, t * 128:(t + 1) * 128], [f"QB{qb}"], ["R_KT"])

        def evac_Bv(t, pb):
            s, tok0 = t // 2, (t % 2) * 128
            if isP:
                vo = cnt["vo"] % 2
                cnt["vo"] += 1
                op("act", lambda e: e.copy(out=VO[vo][:], in_=bank(pb)), reads=[bk(pb)], writes=[f"VO{vo}"])
                dma("sp", d_nvb[s, l, tok0:tok0 + 128, :], VO[vo][:], reads=[f"VO{vo}"])
                op("act", lambda e: e.copy(out=VA[:, t, :, 0:64], in_=v3(VO[vo][:], 8)), reads=[f"VO{vo}"], writes=["R_VA"])
            else:
                op("act", lambda e: e.copy(out=VA[:, t, :, 0:64], in_=v3(bank(pb), 8)), reads=[bk(pb)], writes=["R_VA"])

        proj_tok(items["Bq"], 512, evac_Bq)
        proj_tok(items["Bk"], 512, evac_Bk)
        proj_tok(items["Bv"], 512, evac_Bv)
        proj_tok(items["Bg"], 512, evac_g)
        flush_pend()

        units = []
        if isP:
            for s in range(4):
                for h in range(8):
                    j, b0 = h // 2, (h % 2) * 64
                    chunks = [dict(kT=KT[b0:b0 + 64, j, s * 256 + kc * 128:s * 256 + (kc + 1) * 128], v=VA[:, 2 * s + kc, h, :]) for kc in range(2)]
                    slots = [dict(col=qt * 128, y=GT[:, 2 * s + qt, h * 64:(h + 1) * 64], ykeys=[f"R_GT{2 * s + qt}"]) for qt in range(2)]
                    units.append(dict(ncols=256, q=QT[b0:b0 + 64, j, s * 256:(s + 1) * 256], banks=pack_banks(256, [], None, chunks), slots=slots, esink=None,
                                      rkeys=["R_QT", "R_KT", "R_VA"],
                                      after=(lambda s=s: [y_transposes(2 * s + qt, 1) for qt in range(2)]) if h == 7 else None))
        else:
            for i in range(NT):
                nbc = nb_chunks(i)
                for h in range(8):
                    j, b0 = h // 2, (h % 2) * 64
                    Gh = G[:, h, :, :].rearrange("p a b -> p (a b)")
                    loc = sorted(nbc, key=lambda ck: -ck[0])
                    biased = [dict(kT=KT[b0:b0 + 64, j, c * 128:(c + 1) * 128], v=VA[:, c, h, :]) for (c, kind) in loc]
                    offs = [(7 - 2 * (c - i)) * 64 for (c, kind) in loc]
                    posts = [(k, RMASK[:, kind, :]) for k, (c, kind) in enumerate(loc) if kind is not None]
                    plain = [dict(kT=KCT[b0:b0 + 64, j, c * 128:(c + 1) * 128], v=VC[:, c, h, :]) for c in range(4)]
                    banks = pack_banks(128, biased, (lambda k0, n, offs=offs, Gh=Gh: Gh[:, offs[k0]:offs[k0] + n * 128]), plain, posts)
                    slots = [dict(col=0, y=GT[:, i, h * 64:(h + 1) * 64], ykeys=[f"R_GT{i}"])]
                    units.append(dict(ncols=128, q=QT[b0:b0 + 64, j, i * 128:(i + 1) * 128], banks=banks, slots=slots, esink=None,
                                      rkeys=["R_QT", "R_KT", "R_VA", "R_KCT", "R_VC"],
                                      after=(lambda i=i: y_transposes(i, 1)) if h == 7 else None))
        attend(units, l)
        dbg('yTB', yT[1][:, :, :], ['yT1'])

        if ada_n[0] < 24:
            ada_rows(1)
        nseg = 4 if isP else 1
        L = T // nseg
        op("pool", lambda e: e.memset(RG[:, :], 0.0), writes=["XP0"])
        op("pool", lambda e: e.memset(CSTr[:, :], 0.0), writes=["XP1"])
        cx_slot, cx_key = ws.acquire(items["Cx"])
        cg_slot, cg_key = ws.acquire(items["Cg"], base=items["Cx"])

        def featmm(slot, skey, jc, th, pb):
            for kc in range(8):
                op("pe", lambda e, kc=kc: e.matmul(bank(pb), lhsT=slot[:, kc, jc * 128:(jc + 1) * 128], rhs=hT[:, kc, th * 512:(th + 1) * 512],
                                                   start=(kc == 0), stop=(kc == 7)), reads=["hT", skey], writes=[bk(pb)])

        def seg(ap):
            return ap.rearrange("p (s x) -> p s x", s=nseg)

        def lru_front_a(jc, fs):
            XPf = XPr[fs][:, 0:nseg * (L + 3)].rearrange("p (s x) -> p s x", s=nseg)
            XCf, XCbf = XCs[fs], XCbs[fs]
            kxp, kxc, kxb = f"XP{fs}", f"XC{fs}", f"XCb{fs}"
            for th in range(2):
                pb = next_pj()
                featmm(cx_slot, cx_key, jc, th, pb)
                if isP:
                    dst = XPf[:, 2 * th:2 * th + 2, 2:2 + L]
                    src = bank(pb).rearrange("p (s x) -> p s x", s=2)
                else:
                    dst = XPf[:, 0, 2 + th * 512:2 + (th + 1) * 512]
                    src = bank(pb)
                op("act", lambda e, dst=dst, src=src: e.copy(out=dst, in_=src), reads=[bk(pb)], writes=[kxp])
            op("dve", lambda e: e.tensor_scalar(out=seg(XCf), in0=XPf[:, :, 0:L], scalar1=CW[:, l, 0, jc:jc + 1], scalar2=CB[:, l, jc:jc + 1],
                                                op0=ALU.mult, op1=ALU.add), reads=[kxp, "lrup"], writes=[kxc])
            for j in range(1, 4):
                op("dve", lambda e, j=j: e.scalar_tensor_tensor(out=seg(XCf), in0=XPf[:, :, j:j + L], scalar=CW[:, l, j, jc:jc + 1], in1=seg(XCf),
                                                                op0=ALU.mult, op1=ALU.add), reads=[kxp, kxc, "lrup"], writes=[kxc])
            op("pool", lambda e: e.tensor_copy(out=XCbf, in_=XCf), reads=[kxc], writes=[kxb])
            if fs == 0:
                dbg('XC', XCf, [kxc])

        def lru_front_b(jc, fs):
            SGf = SGs[fs]
            ksg = f"SG{fs}"
            for th in range(2):
                pb = next_pj()
                featmm(cg_slot, cg_key, jc, th, pb)
                op("act", lambda e, th=th, pb=pb: e.activation(out=SGf[:, th * 512:(th + 1) * 512], in_=bank(pb), func=AF.Silu),
                   reads=[bk(pb)], writes=[ksg])
            if fs == 0:
                dbg('SG', SGf, [ksg])

        def lru_back(jc, fs):
            XCf, XCbf, SGf = XCs[fs], XCbs[fs], SGs[fs]
            kxc, kxb, ksg = f"XC{fs}", f"XCb{fs}", f"SG{fs}"
            for dr in range(2):
                for ty, (dst, biash) in enumerate(((LR[dr], BAh), (LI[dr], BXh))):
                    pp = cnt["pss"] % 2
                    cnt["pss"] += 1
                    b_lo = 4 + 2 * pp
                    region = PS[:, b_lo:b_lo + 2, :].rearrange("p a b -> p (a b)")
                    rk = [bk(b_lo), bk(b_lo + 1)]
                    for th in range(2):
                        op("pe", lambda e, th=th, ty=ty, dr=dr, region=region: e.matmul(region[:, th * 512:(th + 1) * 512], lhsT=BD[:, ty, dr, jc, :],
                                                                                  rhs=XCbf[:, th * 512:(th + 1) * 512], start=True, stop=True),
                           reads=["BD", kxb], writes=rk)
                    op("act", lambda e, dst=dst, biash=biash, dr=dr, region=region: e.activation(
                        out=dst[:, :], in_=region, func=AF.Tanh, scale=0.5, bias=biash[:, l, dr, jc:jc + 1]),
                       reads=rk + ["lrup3"], writes=[f"L{ty}{dr}"])
            for dr in range(2):
                op("act", lambda e, dr=dr: e.activation(out=LR[dr][:], in_=LR[dr][:], func=AF.Exp, scale=SAh[:, l, dr, jc:jc + 1], bias=SAh[:, l, dr, jc:jc + 1]),
                   reads=[f"L0{dr}", "lrup3"], writes=[f"L0{dr}"])
                op("pool", lambda e, dr=dr: e.tensor_tensor(out=LA2[dr][:], in0=LR[dr][:], in1=LR[dr][:], op=ALU.mult),
                   reads=[f"L0{dr}"], writes=[f"LA2{dr}"])
            for dr in range(2):
                op("act", lambda e, dr=dr: e.activation(out=LA2[dr][:], in_=LA2[dr][:], func=AF.Sqrt, scale=-1.0, bias=ONE[:]),
                   reads=[f"LA2{dr}", "consts"], writes=[f"LA2{dr}"])
            for dr in range(2):
                op("dve", lambda e, dr=dr: e.scalar_tensor_tensor(out=LI[dr][:], in0=LI[dr][:], scalar=1.0, in1=LA2[dr][:], op0=ALU.add, op1=ALU.mult),
                   reads=[f"L1{dr}", f"LA2{dr}"], writes=[f"L1{dr}"])
                op("dve", lambda e, dr=dr: e.scalar_tensor_tensor(out=LI[dr][:], in0=LI[dr][:], scalar=0.5, in1=XCf, op0=ALU.mult, op1=ALU.mult),
                   reads=[f"L1{dr}", kxc], writes=[f"L1{dr}"])
                for s_ in range(nseg):
                    a_ap = LR[dr][:, s_ * L:(s_ + 1) * L]
                    u_ap = LI[dr][:, s_ * L:(s_ + 1) * L]
                    o_ap = LH[dr][:, s_ * L:(s_ + 1) * L]
                    init = 0.0 if isP else H0[:, l, dr, jc:jc + 1]
                    if dr == 0:
                        op("dve", lambda e, a_ap=a_ap, u_ap=u_ap, o_ap=o_ap, init=init: e.tensor_tensor_scan(
                            out=o_ap, data0=a_ap, data1=u_ap, initial=init, op0=ALU.mult, op1=ALU.add),
                           reads=[f"L0{dr}", f"L1{dr}", "lrup"], writes=[f"LH{dr}"])
                    else:
                        op("dve", lambda e, a_ap=a_ap, u_ap=u_ap, o_ap=o_ap, init=init: e.tensor_tensor_scan(
                            out=o_ap[:, ::-1], data0=a_ap[:, ::-1], data1=u_ap[:, ::-1], initial=init, op0=ALU.mult, op1=ALU.add),
                           reads=[f"L0{dr}", f"L1{dr}", "lrup"], writes=[f"LH{dr}"])
                if isP:
                    col = L - 1 if dr == 0 else 0
                    op("dve", lambda e, dr=dr, col=col: e.tensor_copy(out=STO[:, :, l, dr, jc], in_=seg(LH[dr][:])[:, :, col]),
                       reads=[f"LH{dr}"], writes=["STO"])
            if fs == 0:
                dbg('A0', LR[0][:, :], ['L00'])
                dbg('U0', LI[0][:, :], ['L10'])
                dbg('HF0', LH[0][:, :], ['LH0'])
                dbg('HB0', LH[1][:, :], ['LH1'])
            op("dve", lambda e: e.tensor_tensor(out=LH[0][:], in0=LH[0][:], in1=LH[1][:], op=ALU.add), reads=["LH0", "LH1"], writes=["LH0"])
            op("dve", lambda e: e.tensor_tensor(out=yT[2][:, jc, :], in0=LH[0][:], in1=SGf, op=ALU.mult), reads=["LH0", ksg], writes=["yT2"])

        lru_front_a(0, 0)
        lru_front_b(0, 0)
        lru_front_a(1, 1)
        for jc in range(4):
            lru_back(jc, jc % 2)
            if jc + 1 < 4:
                lru_front_b(jc + 1, (jc + 1) % 2)
            if jc + 2 < 4:
                lru_front_a(jc + 2, jc % 2)

        dbg('yTC', yT[2][:, :, :], ['yT2'])
        dma("sp", MODB[:, 0:D], dram_bcast(d_modd[l, r, 2 * D:3 * D]), reads=["modd"], writes=["MODB_SH"])
        ZA = [RC[:, 0:1024], RC[:, 1024:2048], RF[:, 0:1024], RG[:, 0:1024]]
        mctr = 0
        for ocg in range(2):
            for b in range(3):
                dma("pool", WBR[:, b, :, :], d_wbr[l, b][:, ocg * 512:(ocg + 1) * 512].rearrange("(k p) c -> p k c", p=128), writes=[f"WBR{b}"])
            for b in range(3):
                gslot, gkey = ws.acquire(items[("G", ocg, b)])
                for oc4 in range(4):
                    oc = ocg * 4 + oc4
                    for th in range(2):
                        pm = 2 * (mctr % 4)
                        pg = pm + 1
                        for kc in range(4):
                            op("pe", lambda e, kc=kc, b=b, pm=pm: e.matmul(bank(pm), lhsT=WBR[:, b, kc, oc4 * 128:(oc4 + 1) * 128],
                                                                          rhs=yT[b][:, kc, th * 512:(th + 1) * 512], start=(kc == 0), stop=(kc == 3)),
                               reads=[f"WBR{b}", f"yT{b}"], writes=[bk(pm)])
                        for kc in range(8):
                            op("pe", lambda e, kc=kc, gslot=gslot, pg=pg: e.matmul(bank(pg), lhsT=gslot[:, kc, oc4 * 128:(oc4 + 1) * 128],
                                                                                  rhs=hT[:, kc, th * 512:(th + 1) * 512], start=(kc == 0), stop=(kc == 7)),
                               reads=[gkey, "hT"], writes=[bk(pg)])
                        ms = mctr % 2
                        SQ = SQs[ms]
                        op("act", lambda e, pg=pg: e.activation(out=SQ[:], in_=bank(pg), func=AF.Sigmoid), reads=[bk(pg)], writes=[f"SQ{ms}"])
                        za = ZA[oc4][:, th * 512:(th + 1) * 512]
                        zak = f"ZA{oc4}"
                        if b == 0:
                            op("dve", lambda e, pm=pm, za=za: e.tensor_tensor(out=za, in0=bank(pm), in1=SQ[:], op=ALU.mult), reads=[bk(pm), f"SQ{ms}"], writes=[zak])
                        elif b == 1:
                            QR = QRs[ms]
                            op("dve", lambda e, pm=pm: e.tensor_tensor(out=QR[:], in0=bank(pm), in1=SQ[:], op=ALU.mult), reads=[bk(pm), f"SQ{ms}"], writes=[f"QR{ms}"])
                            op("pool", lambda e, za=za: e.tensor_tensor(out=za, in0=za, in1=QR[:], op=ALU.add), reads=[zak, f"QR{ms}"], writes=[zak])
                        else:
                            QR = QRs[ms]
                            op("dve", lambda e, pm=pm: e.tensor_tensor(out=QR[:], in0=bank(pm), in1=SQ[:], op=ALU.mult), reads=[bk(pm), f"SQ{ms}"], writes=[f"QR{ms}"])
                            zk = "R_QT" if oc < 4 else "R_KT"
                            op("pool", lambda e, oc=oc, th=th, za=za: e.tensor_tensor(out=zT[oc // 4][:, oc % 4, th * 512:(th + 1) * 512], in0=za, in1=QR[:], op=ALU.add),
                               reads=[zak, f"QR{ms}"], writes=[zk])
                        mctr += 1

        oslots = [ws.acquire(items[("O", 0)]), ws.acquire(items[("O", 1)], base=items[("O", 0)])]
        for t in range(NT):
            for cg in range(2):
                oslot, okey = oslots[cg]
                pb = next_pj()
                for kc in range(8):
                    op("pe", lambda e, kc=kc, t=t, pb=pb, oslot=oslot: e.matmul(bank(pb), lhsT=zT[kc // 4][:, kc % 4, t * 128:(t + 1) * 128], rhs=oslot[:, kc, :],
                                                                              start=(kc == 0), stop=(kc == 7)), reads=["R_QT", "R_KT", okey], writes=[bk(pb)])
                os_ = (2 * t + cg) % 2
                op("dve", lambda e, pb=pb, cg=cg, os_=os_: e.tensor_tensor(out=QNs[os_][:], in0=bank(pb), in1=GATEB[:, cg * 512:(cg + 1) * 512], op=ALU.mult),
                   reads=[bk(pb), "MODB_SH"], writes=[f"QN{os_}"])
                op("pool", lambda e, t=t, cg=cg, os_=os_: e.tensor_tensor(out=X[:, t, cg * 512:(cg + 1) * 512], in0=X[:, t, cg * 512:(cg + 1) * 512], in1=QNs[os_][:], op=ALU.add),
                   reads=[f"QN{os_}", f"X{t}"], writes=[f"X{t}"])

    with nc.allow_non_contiguous_dma(reason="small parameter / layout loads"):
        pi = 0
        for hv in halves:
            src = d_xp if hv == "P" else d_xs
            dst = d_yp if hv == "P" else d_ys
            for t in range(NT):
                dma("sp", X[:, t, :], src[t * 128:(t + 1) * 128, :], writes=[f"X{t}"])
            for l in range(2):
                layer_pass(hv, l, pass_items[pi])
                pi += 1
            for t in range(NT):
                dma("sp", dst[t * 128:(t + 1) * 128, :], X[:, t, :], reads=[f"X{t}"])
            if hv == "P":
                def emit_nst():
                    for s in range(4):
                        for l in range(2):
                            for dr in range(2):
                                dma("sp", d_nst[s, l, dr].rearrange("(c p) -> p c", p=128), STO[:, s, l, dr, :], reads=["STO"])
                DEFER.append(emit_nst)
        while DEFER:
            DEFER.pop(0)()
        kb.finish("sp")
    if needed is not None:
        print("ops", kb.nops, "waits", kb.nwaits, "per-engine", kb.cnt)
    return nc, kb


_CONSTS = None


def kernel(x_prompt, x_sample, cache_ka, cache_va, cache_kb, cache_vb, state_lru, c, c_ctx,
           norm_g, w_ada, b_ada, w_in, a_q_norm, a_k_norm, a_sink, b_q_norm, b_k_norm, b_rpb,
           lru_conv_w, lru_conv_b, lru_wa, lru_ba, lru_wx, lru_bx, lru_lambda, w_branch, w_out, _halves=HALVES):
    global _CONSTS
    f = lambda a: np.ascontiguousarray(np.asarray(a, dtype=np.float32))
    if _CONSTS is None:
        _CONSTS = host_consts()
    cst = _CONSTS
    x_prompt, x_sample = f(x_prompt), f(x_sample)
    cache_ka, cache_va, cache_kb, cache_vb = f(cache_ka), f(cache_va), f(cache_kb), f(cache_vb)
    state_lru, c, c_ctx = f(state_lru), f(c), f(c_ctx)
    rpb = f(b_rpb).reshape(-1)
    rpbpad = np.concatenate([np.zeros(64, np.float32), rpb, np.zeros(256 - 64, np.float32)])
    gqk = np.stack([f(a_q_norm), f(a_k_norm), f(b_q_norm), f(b_k_norm)], axis=0)
    shared = {
        "norm_g": f(norm_g), "w_ada": f(w_ada), "b_ada": f(b_ada), "w_in": f(w_in), "gqk": gqk, "a_sink": f(a_sink),
        "rpbpad": rpbpad, "conv_w": f(lru_conv_w), "conv_b": f(lru_conv_b), "lru_wa": f(lru_wa), "lru_ba": f(lru_ba),
        "lru_wx": f(lru_wx), "lru_bx": f(lru_bx), "lru_lam": f(lru_lambda), "w_br": f(w_branch), "w_out": f(w_out),
        "ident": cst["ident"], "ropec": cst["ropec"], "ropes": cst["ropes"], "wmask": cst["wmask"], "colm": cst["colm"],
        "rmask": cst["rmask"],
    }
    in_maps = []
    for i in range(NCORES):
        m = dict(shared)
        m["xp"] = x_prompt[4 * i:4 * i + 4].reshape(T, D)
        m["xs"] = x_sample[i].reshape(T, D)
        m["cka"] = cache_ka[i].reshape(2, 512, 128)
        m["cva"] = cache_va[i].reshape(2, 512, 128)
        m["ckb"] = cache_kb[i].reshape(2, 512, 512)
        m["cvb"] = cache_vb[i].reshape(2, 512, 512)
        m["st"] = state_lru[i]
        m["cond"] = np.stack([c_ctx, c[i]], axis=0)
        in_maps.append(m)
    nc = build_program(_halves)
    res = run_bass_kernel_spmd(nc, in_maps, core_ids=list(range(NCORES)))
    rs = res.results
    y_prompt = np.concatenate([r["yp"].reshape(4, 256, D) for r in rs], axis=0)
    y_sample = np.stack([r["ys"].reshape(T, D) for r in rs], axis=0)
    nka = np.concatenate([r["nka"].reshape(4, 2, 256, 2, 64) for r in rs], axis=0)
    nva = np.concatenate([r["nva"].reshape(4, 2, 256, 2, 64) for r in rs], axis=0)
    nkb = np.concatenate([r["nkb"].reshape(4, 2, 256, 8, 64) for r in rs], axis=0)
    nvb = np.concatenate([r["nvb"].reshape(4, 2, 256, 8, 64) for r in rs], axis=0)
    nst = np.concatenate([r["nst"].reshape(4, 2, 2, 512) for r in rs], axis=0)
    out = (y_prompt, y_sample, nka, nva, nkb, nvb, nst)
    return tuple(np.ascontiguousarray(o, dtype=np.float32) for o in out)
```
